# Optimizing a Trainium2 kernel written in Bass

```python
import math
import jax, jax.numpy as jnp
from jax import lax
import numpy as np

D_MODEL = 1024
BATCH = 32
SEQ = 256
DEPTH = 2
DEC_BATCH = 4
DEC_SEQ = 2048
PAST_LEN = 512

GRID_W = 64
HEAD_DIM = 64
N_Q_HEADS = 8
N_KV_HEADS = 2
Q_PER_KV = N_Q_HEADS // N_KV_HEADS
D_ATTN = N_Q_HEADS * HEAD_DIM
D_KV = N_KV_HEADS * HEAD_DIM
D_SSM = D_MODEL - D_ATTN
SSM_GROUP = 16
N_SSM_GROUPS = D_SSM // SSM_GROUP
SSM_STATE = 64
D_IN = D_ATTN + 2 * D_KV + D_SSM
D_MIX = D_ATTN + D_SSM
D_FF = ((8 * D_MODEL // 3 + 255) // 256) * 256
AXIS_DIM = HEAD_DIM // 2
ROPE_THETA = 10000.0
Q_BLOCK = 128
N_MOD = 6
EPS = 1e-6

kernel_name = "hymba_s5_gqa_prefix_dit_step"


def rmsnorm(x, g):
    xf = x.astype(jnp.float32)
    y = xf * lax.rsqrt(jnp.mean(xf * xf, axis=-1, keepdims=True) + EPS)
    return (y * g.astype(jnp.float32)).astype(x.dtype)


def axial_rope(n_tokens):
    rows = n_tokens // GRID_W
    row = jnp.repeat(jnp.arange(rows, dtype=jnp.float32), GRID_W)
    col = jnp.tile(jnp.arange(GRID_W, dtype=jnp.float32), rows)
    inv_freq = ROPE_THETA ** (-jnp.arange(0, AXIS_DIM, 2, dtype=jnp.float32) / AXIS_DIM)
    ang = jnp.concatenate([row[:, None] * inv_freq, col[:, None] * inv_freq], axis=-1)
    return jnp.cos(ang), jnp.sin(ang)


def apply_rope(x, cos, sin):
    b, l, h, d = x.shape
    xp = x.reshape(b, l, h, d // 2, 2)
    x0, x1 = xp[..., 0], xp[..., 1]
    cs = cos[None, :, None, :].astype(x.dtype)
    sn = sin[None, :, None, :].astype(x.dtype)
    out = jnp.stack([x0 * cs - x1 * sn, x0 * sn + x1 * cs], axis=-1)
    return out.reshape(b, l, h, d)


def block_attention(q, k, v):
    b, l = q.shape[0], q.shape[1]
    nb = l // Q_BLOCK
    qb = q.reshape(b, nb, Q_BLOCK, N_KV_HEADS, Q_PER_KV, HEAD_DIM).transpose(1, 0, 2, 3, 4, 5)
    scale = HEAD_DIM ** -0.5

    def one_block(q_blk):
        s = jnp.einsum("bqkgd,bskd->bkgqs", q_blk, k, preferred_element_type=jnp.float32) * scale
        p = jax.nn.softmax(s, axis=-1).astype(v.dtype)
        return jnp.einsum("bkgqs,bskd->bqkgd", p, v)

    out = lax.map(one_block, qb)
    return out.transpose(1, 0, 2, 3, 4, 5).reshape(b, l, D_ATTN)


def ssm_discretise(a_re, a_im, log_dt, b_re, b_im):
    dt = jnp.exp(log_dt.astype(jnp.float32))[:, None]
    ar = a_re.astype(jnp.float32)
    ai = a_im.astype(jnp.float32)
    mag = jnp.exp(ar * dt)
    lr = mag * jnp.cos(ai * dt)
    li = mag * jnp.sin(ai * dt)
    den = ar * ar + ai * ai
    zr = ((lr - 1.0) * ar + li * ai) / den
    zi = (li * ar - (lr - 1.0) * ai) / den
    br = b_re.astype(jnp.float32)
    bi = b_im.astype(jnp.float32)
    bbar_re = zr[..., None] * br - zi[..., None] * bi
    bbar_im = zr[..., None] * bi + zi[..., None] * br
    return lr, li, bbar_re, bbar_im


def _combine(e1, e2):
    a1r, a1i, b1r, b1i = e1
    a2r, a2i, b2r, b2i = e2
    return (a2r * a1r - a2i * a1i, a2r * a1i + a2i * a1r,
            a2r * b1r - a2i * b1i + b2r, a2r * b1i + a2i * b1r + b2i)


def ssm_direction(u, a_re, a_im, log_dt, b_re, b_im, c_re, c_im, h0, reverse):
    lr, li, bbr, bbi = ssm_discretise(a_re, a_im, log_dt, b_re, b_im)
    xr = jnp.einsum("blgh,gph->blgp", u, bbr)
    xi = jnp.einsum("blgh,gph->blgp", u, bbi)
    if h0 is not None:
        first = -1 if reverse else 0
        h0r, h0i = h0
        xr = xr.at[:, first].add(lr * h0r - li * h0i)
        xi = xi.at[:, first].add(lr * h0i + li * h0r)
    ar = jnp.broadcast_to(lr, xr.shape)
    ai = jnp.broadcast_to(li, xr.shape)
    _, _, hr, hi = lax.associative_scan(_combine, (ar, ai, xr, xi), reverse=reverse, axis=1)
    y = (jnp.einsum("blgp,ghp->blgh", hr, c_re.astype(jnp.float32))
         - jnp.einsum("blgp,ghp->blgh", hi, c_im.astype(jnp.float32)))
    return y, hr, hi


def ssm_mixer(u, p, h0, return_state):
    b, l, _ = u.shape
    uf = u.astype(jnp.float32).reshape(b, l, N_SSM_GROUPS, SSM_GROUP)
    y_sum = None
    finals = []
    for d, reverse in ((0, False), (1, True)):
        init = None if h0 is None else (h0[:, d, 0].astype(jnp.float32), h0[:, d, 1].astype(jnp.float32))
        y, hr, hi = ssm_direction(uf, p["a_re"][d], p["a_im"][d], p["log_dt"][d], p["b_re"][d],
                                  p["b_im"][d], p["c_re"][d], p["c_im"][d], init, reverse)
        y_sum = y if y_sum is None else y_sum + y
        if return_state:
            last = 0 if reverse else -1
            finals.append(jnp.stack([hr[:, last], hi[:, last]], axis=1))
    y = y_sum.reshape(b, l, D_SSM) + p["d_skip"].astype(jnp.float32) * uf.reshape(b, l, D_SSM)
    g = jax.nn.gelu(y, approximate=False).astype(u.dtype)
    out = g * jax.nn.sigmoid(g @ p["w_glu"] + p["b_glu"])
    if return_state:
        return out, jnp.stack(finals, axis=1)
    return out


def adaln(cond_act, w_mod, b_mod):
    m = (cond_act @ w_mod + b_mod).reshape(cond_act.shape[0], 1, N_MOD, D_MODEL)
    return tuple(m[:, :, i] for i in range(N_MOD))


def trunk_layer(x, mods, p, rope=None, ctx_k=None, ctx_v=None, h0=None):
    shift1, scale1, gate1, shift2, scale2, gate2 = mods
    b, l, _ = x.shape
    h = rmsnorm(x, p["norm1"]) * (1.0 + scale1) + shift1
    proj = h @ p["w_in"]
    q, k, v, u = jnp.split(proj, [D_ATTN, D_ATTN + D_KV, D_ATTN + 2 * D_KV], axis=-1)
    q = rmsnorm(q.reshape(b, l, N_Q_HEADS, HEAD_DIM), p["q_norm"])
    k = rmsnorm(k.reshape(b, l, N_KV_HEADS, HEAD_DIM), p["k_norm"])
    v = v.reshape(b, l, N_KV_HEADS, HEAD_DIM)
    is_context = rope is None
    if is_context:
        attn = block_attention(q, k, v)
        ssm_out, state = ssm_mixer(u, p, None, True)
    else:
        cos, sin = rope
        q = apply_rope(q, cos, sin)
        k_lat = apply_rope(k, cos, sin)
        k_all = jnp.concatenate([ctx_k.astype(k.dtype), k_lat], axis=1)
        v_all = jnp.concatenate([ctx_v.astype(v.dtype), v], axis=1)
        attn = block_attention(q, k_all, v_all)
        ssm_out = ssm_mixer(u, p, h0, False)
    mixed = jnp.concatenate([attn, ssm_out], axis=-1) @ p["w_out"]
    x = x + gate1 * mixed
    h2 = rmsnorm(x, p["norm2"]) * (1.0 + scale2) + shift2
    gate, up = jnp.split(h2 @ p["w_ffn_in"], [D_FF], axis=-1)
    x = x + gate2 * ((jax.nn.silu(gate) * up) @ p["w_ffn_out"])
    if is_context:
        return x, k, v, state
    return x


def setup_inputs(seed: int = 0) -> dict:
    key = jax.random.key(seed)
    ks = iter(jax.random.split(key, 40))
    f32 = jnp.float32
    G, P, H = N_SSM_GROUPS, SSM_STATE, SSM_GROUP

    def nrm(shape, scale):
        return jax.random.normal(next(ks), shape, f32) * scale

    n_idx = jnp.arange(P, dtype=f32)
    return {
        "x_prompt": nrm((BATCH, SEQ, D_MODEL), 1.0),
        "x_sample": nrm((DEC_BATCH, DEC_SEQ, D_MODEL), 1.0),
        "cache_k": nrm((DEC_BATCH, DEPTH, PAST_LEN, N_KV_HEADS, HEAD_DIM), 1.0),
        "cache_v": nrm((DEC_BATCH, DEPTH, PAST_LEN, N_KV_HEADS, HEAD_DIM), 1.0),
        "state_ssm": nrm((DEC_BATCH, DEPTH, 2, 2, G, P), 0.5),
        "c": nrm((DEC_BATCH, D_MODEL), 1.0),
        "c_ctx": nrm((D_MODEL,), 1.0),
        "w_mod": nrm((DEPTH, D_MODEL, N_MOD * D_MODEL), 0.5 * D_MODEL ** -0.5),
        "b_mod": nrm((DEPTH, N_MOD * D_MODEL), 0.01),
        "norm1": 1.0 + nrm((DEPTH, D_MODEL), 0.02),
        "norm2": 1.0 + nrm((DEPTH, D_MODEL), 0.02),
        "w_in": nrm((DEPTH, D_MODEL, D_IN), D_MODEL ** -0.5),
        "q_norm": 1.0 + nrm((DEPTH, HEAD_DIM), 0.02),
        "k_norm": 1.0 + nrm((DEPTH, HEAD_DIM), 0.02),
        "ssm_a_re": -0.5 + nrm((DEPTH, 2, G, P), 0.01),
        "ssm_a_im": math.pi * n_idx + nrm((DEPTH, 2, G, P), 0.01),
        "ssm_log_dt": jax.random.uniform(next(ks), (DEPTH, 2, G), f32, math.log(1e-3), math.log(1e-1)),
        "ssm_b_re": nrm((DEPTH, 2, G, P, H), H ** -0.5),
        "ssm_b_im": nrm((DEPTH, 2, G, P, H), H ** -0.5),
        "ssm_c_re": nrm((DEPTH, 2, G, H, P), 0.5 * P ** -0.5),
        "ssm_c_im": nrm((DEPTH, 2, G, H, P), 0.5 * P ** -0.5),
        "ssm_d": nrm((DEPTH, D_SSM), 0.5),
        "w_glu": nrm((DEPTH, D_SSM, D_SSM), D_SSM ** -0.5),
        "b_glu": nrm((DEPTH, D_SSM), 0.01),
        "w_out": nrm((DEPTH, D_MIX, D_MODEL), D_MIX ** -0.5),
        "w_ffn_in": nrm((DEPTH, D_MODEL, 2 * D_FF), D_MODEL ** -0.5),
        "w_ffn_out": nrm((DEPTH, D_FF, D_MODEL), D_FF ** -0.5),
        "final_norm": 1.0 + nrm((D_MODEL,), 0.02),
    }


def reference(x_prompt, x_sample, cache_k, cache_v, state_ssm, c, c_ctx, w_mod, b_mod, norm1, norm2,
              w_in, q_norm, k_norm, ssm_a_re, ssm_a_im, ssm_log_dt, ssm_b_re, ssm_b_im, ssm_c_re,
              ssm_c_im, ssm_d, w_glu, b_glu, w_out, w_ffn_in, w_ffn_out, final_norm):
    rope = axial_rope(x_sample.shape[1])
    cond_ctx = jax.nn.silu(c_ctx)[None, :]
    cond_lat = jax.nn.silu(c)
    xp, xs = x_prompt, x_sample
    new_k, new_v, new_s = [], [], []
    for l in range(DEPTH):
        p = {
            "norm1": norm1[l], "norm2": norm2[l], "w_in": w_in[l],
            "q_norm": q_norm[l], "k_norm": k_norm[l],
            "a_re": ssm_a_re[l], "a_im": ssm_a_im[l], "log_dt": ssm_log_dt[l],
            "b_re": ssm_b_re[l], "b_im": ssm_b_im[l], "c_re": ssm_c_re[l], "c_im": ssm_c_im[l],
            "d_skip": ssm_d[l], "w_glu": w_glu[l], "b_glu": b_glu[l], "w_out": w_out[l],
            "w_ffn_in": w_ffn_in[l], "w_ffn_out": w_ffn_out[l],
        }
        xp, k_ctx, v_ctx, s_ctx = trunk_layer(xp, adaln(cond_ctx, w_mod[l], b_mod[l]), p)
        new_k.append(k_ctx)
        new_v.append(v_ctx)
        new_s.append(s_ctx)
        xs = trunk_layer(xs, adaln(cond_lat, w_mod[l], b_mod[l]), p, rope=rope,
                         ctx_k=cache_k[:, l], ctx_v=cache_v[:, l], h0=state_ssm[:, l])
    y_prompt = rmsnorm(xp, final_norm)
    y_sample = rmsnorm(xs, final_norm)
    new_cache_k = jnp.stack(new_k, axis=1)
    new_cache_v = jnp.stack(new_v, axis=1)
    new_state_ssm = jnp.stack(new_s, axis=1)
    return (y_prompt, y_sample, new_cache_k, new_cache_v, new_state_ssm)
```

```python
import numpy as np
import contextlib
import concourse.bass as bass
import concourse.mybir as mybir
from concourse.bass_utils import run_bass_kernel_spmd

F32 = mybir.dt.float32
BF16 = mybir.dt.bfloat16
U8 = mybir.dt.uint8
ALU = mybir.AluOpType
AF = mybir.ActivationFunctionType

ENGS = ("pe", "act", "dve", "pool", "sp")
SYNC_SAME = {"pool": True, "act": True, "dve": True, "sp": True, "pe": False}

D_MODEL = 1024
DEPTH = 2
NT = 2048
NCH = 256
D_FF = 2816
NKEY = 2560
EPS = 1e-6
BIG = 32768.0


class Op:
    __slots__ = ("eng", "fn", "reads", "writes", "dma", "key", "idx", "waits", "signal", "cnt", "bar")

    def __init__(self, eng, fn, reads, writes, dma, key):
        self.eng, self.fn, self.reads, self.writes, self.dma, self.key = eng, fn, reads, writes, dma, key
        self.waits = {}
        self.signal = False
        self.cnt = 0
        self.bar = False


class Prog:
    def __init__(self, nc):
        self.nc = nc
        self.ops = []

    def op(self, eng, fn, reads=(), writes=()):
        o = Op(eng, fn, tuple(reads), tuple(writes), False, None)
        o.idx = len(self.ops)
        self.ops.append(o)
        return o

    def dma(self, eng, fn, reads=(), writes=(), key=None):
        assert key is not None
        o = Op(eng, fn, tuple(reads), tuple(writes), True, key)
        o.idx = len(self.ops)
        self.ops.append(o)
        return o

    def barrier(self):
        o = Op(None, None, (), (), False, None)
        o.bar = True
        o.idx = len(self.ops)
        self.ops.append(o)

    @staticmethod
    def _skip_same(do, o):
        return do.eng == o.eng and (not SYNC_SAME[do.eng] or (do.eng == "pe" and not o.dma))

    def resolve(self):
        last_w, readers = {}, {}
        deps_of = []
        for o in self.ops:
            deps = set()
            if not o.bar:
                for b in o.reads:
                    if b in last_w:
                        deps.add(last_w[b])
                for b in o.writes:
                    if b in last_w:
                        deps.add(last_w[b])
                    for r in readers.get(b, ()):
                        deps.add(r)
                deps.discard(o.idx)
                for b in o.reads:
                    readers.setdefault(b, []).append(o.idx)
                for b in o.writes:
                    last_w[b] = o.idx
                    readers[b] = []
            deps_of.append(deps)
        last_nd = {e: None for e in ENGS}
        for o in self.ops:
            if o.bar:
                for e in ENGS:
                    if last_nd[e] is not None:
                        self.ops[last_nd[e]].signal = True
                continue
            if not o.dma:
                last_nd[o.eng] = o.idx
            for d in deps_of[o.idx]:
                do = self.ops[d]
                if do.dma or self._skip_same(do, o):
                    continue
                do.signal = True
        eng_cnt = {e: 0 for e in ENGS}
        key_running = {}
        pending = {e: {} for e in ENGS}
        for o in self.ops:
            if o.bar:
                bw = {("eng", e): eng_cnt[e] for e in ENGS if eng_cnt[e] > 0}
                for k, v in key_running.items():
                    bw[("dma", k)] = v
                for e in ENGS:
                    for k, v in bw.items():
                        if k == ("eng", e):
                            continue
                        pending[e][k] = max(pending[e].get(k, 0), v)
                continue
            if o.dma:
                key_running[o.key] = key_running.get(o.key, 0) + 16
                o.cnt = key_running[o.key]
            elif o.signal:
                eng_cnt[o.eng] += 1
                o.cnt = eng_cnt[o.eng]
            waits = dict(pending[o.eng])
            pending[o.eng] = {}
            for d in deps_of[o.idx]:
                do = self.ops[d]
                if do.dma:
                    k = ("dma", do.key)
                    v = do.cnt if (o.dma and o.key == do.key) else key_running[do.key]
                    waits[k] = max(waits.get(k, 0), v)
                else:
                    if self._skip_same(do, o):
                        continue
                    k = ("eng", do.eng)
                    waits[k] = max(waits.get(k, 0), do.cnt)
            o.waits = waits
        self.keys = sorted(key_running.keys(), key=str)
        self.key_total = key_running

    def emit(self):
        nc = self.nc
        self.resolve()
        with contextlib.ExitStack() as st:
            sems = {}
            for e in ENGS:
                sems[("eng", e)] = st.enter_context(nc.semaphore("s_" + e))
            for i, k in enumerate(self.keys):
                sems[("dma", k)] = st.enter_context(nc.semaphore("d%d" % i))
            block = st.enter_context(nc.Block())
            engmap = {"pe": block.tensor, "act": block.scalar, "dve": block.vector,
                      "pool": block.gpsimd, "sp": block.sync}
            ops = self.ops
            key_total = self.key_total
            keys = self.keys

            def make(ename):
                def body(eng):
                    waited = {}
                    for o in ops:
                        if o.bar or o.eng != ename:
                            continue
                        for k, v in o.waits.items():
                            if waited.get(k, 0) >= v:
                                continue
                            eng.wait_ge(sems[k], v)
                            waited[k] = v
                        ins = o.fn(eng)
                        if o.dma:
                            ins.then_inc(sems[("dma", o.key)], 16)
                        elif o.signal:
                            ins.then_inc(sems[("eng", o.eng)], 1)
                    if ename == "sp":
                        for k in keys:
                            eng.wait_ge(sems[("dma", k)], key_total[k])
                return body

            for e in ENGS:
                engmap[e](make(e))


class Arena:
    def __init__(self, nc, size):
        self.ap = nc.alloc_sbuf_tensor("arena", [128, size], U8).ap()
        self.size = size
        self.off = 0

    def alloc(self, shape, dt, parts=128):
        esz = 4 if dt == F32 else 2
        n = int(np.prod(shape)) * esz
        off = (self.off + 63) // 64 * 64
        assert off + n <= self.size, ("arena overflow", off, n, self.size)
        self.off = off + n
        v = self.ap[0:parts, off:off + n].bitcast(dt)
        if len(shape) == 2:
            v = v.rearrange("p (a b) -> p a b", a=shape[0])
        elif len(shape) == 3:
            v = v.rearrange("p (a b c) -> p a b c", a=shape[0], b=shape[1])
        elif len(shape) == 4:
            v = v.rearrange("p (a b c d) -> p a b c d", a=shape[0], b=shape[1], c=shape[2])
        elif len(shape) == 5:
            v = v.rearrange("p (a b c d e) -> p a b c d e", a=shape[0], b=shape[1], c=shape[2], d=shape[3])
        return v

    def mark(self):
        return self.off

    def reset(self, m):
        self.off = m


NSM = 480 + 2048 + 288 + 64 + 64


class KB:
    def __init__(self):
        self.nc = bass.Bass("TRN2", target_bir_lowering=False)
        self.P = Prog(self.nc)
        self.AR = Arena(self.nc, 212480)
        self.ps = self.nc.alloc_psum_tensor("ps", [128, 8, 512], F32).ap()
        self.d = {}
        self.uid = 0

    def nm(self, s):
        self.uid += 1
        return "%s#%d" % (s, self.uid)

    def din(self, n, shape):
        self.d[n] = self.nc.dram_tensor(n, list(shape), F32, kind="ExternalInput").ap()
        return self.d[n]

    def dout(self, n, shape):
        self.d[n] = self.nc.dram_tensor(n, list(shape), F32, kind="ExternalOutput").ap()
        return self.d[n]

    def dscr(self, n, shape, dt, ext=False):
        if ext:
            self.d[n] = self.nc.dram_tensor(n, list(shape), dt, kind="ExternalOutput").ap()
        else:
            self.d[n] = self.nc.dram_tensor(n, list(shape), dt).ap()
        return self.d[n]

    def tt(self, eng, out, a, b, op, r, w):
        self.P.op(eng, lambda e: e.tensor_tensor(out=out, in0=a, in1=b, op=op), r, w)

    def ts(self, eng, out, a, s1, op0, r, w, s2=None, op1=None):
        if s2 is None:
            self.P.op(eng, lambda e: e.tensor_scalar(out=out, in0=a, scalar1=s1, scalar2=None, op0=op0), r, w)
        else:
            self.P.op(eng, lambda e: e.tensor_scalar(out=out, in0=a, scalar1=s1, scalar2=s2, op0=op0, op1=op1), r, w)

    def stt(self, out, a, s, b, op0, op1, r, w):
        self.P.op("dve", lambda e: e.scalar_tensor_tensor(out=out, in0=a, scalar=s, in1=b, op0=op0, op1=op1), r, w)

    def cp(self, eng, out, a, r, w):
        self.P.op(eng, lambda e: e.tensor_copy(out=out, in_=a), r, w)

    def act(self, out, a, func, r, w, bias=None, scale=None):
        kw = {}
        if bias is not None:
            kw["bias"] = bias
        if scale is not None:
            kw["scale"] = scale
        self.P.op("act", lambda e: e.activation(out=out, in_=a, func=func, **kw), r, w)

    def mm(self, out, lhsT, rhs, start, stop, r, w):
        self.P.op("pe", lambda e: e.matmul(out, lhsT=lhsT, rhs=rhs, start=start, stop=stop), r, w)

    def mm_hi(self, out, lhsT, rhs, start, stop, r, w):
        self.mm(out[0:64], lhsT[:, 0:64], rhs, start, stop, r, w)
        self.mm(out[64:128], lhsT[:, 64:128], rhs, start, stop, r, w)

    def tr(self, out, a, ident, r, w):
        self.P.op("pe", lambda e: e.transpose(out=out, in_=a, identity=ident), r, w)

    def ms(self, eng, out, val, w):
        self.P.op(eng, lambda e: e.memset(out, val), (), w)

    def rcp(self, out, a, r, w):
        self.P.op("dve", lambda e: e.reciprocal(out=out, in_=a), r, w)

    def dma(self, q, out, a, r, w, key):
        self.P.dma(q, lambda e: e.dma_start(out=out, in_=a), r, w, key)


def bc(ap, shape, axis):
    return ap.unsqueeze(axis).to_broadcast(list(shape))


DEBUG = False


def build():
    kb = KB()
    nc, P, AR, ps, d = kb.nc, kb.P, kb.AR, kb.ps, kb.d
    MUL, ADD, SUB = ALU.mult, ALU.add, ALU.subtract
    for n, s in (("xT", [1024, NT]), ("cond", [128, 8]), ("flag", [128, 1]), ("w_mod", [2, 1024, 6144]),
                 ("b_modT", [2, 128, 48]), ("norm1T", [2, 128, 8]), ("norm2T", [2, 128, 8]), ("fnormT", [128, 8]),
                 ("w_in", [2, 1024, 1280]), ("qkg", [2, 64, 2]), ("a_reT", [2, 2, 128, 16]), ("a_imT", [2, 2, 128, 16]),
                 ("ldtT", [2, 2, 128, 16]), ("b_reT", [2, 2, 128, 16, 16]), ("b_imT", [2, 2, 128, 16, 16]),
                 ("c_reT", [2, 2, 128, 16, 16]), ("c_imT", [2, 2, 128, 16, 16]), ("dS", [2, 128, 32]),
                 ("h0T", [2, 128, 64]), ("w_glu", [2, 512, 512]), ("b_gluT", [2, 128, 4]), ("w_out", [2, 1024, 1024]),
                 ("w_ffn_in", [2, 1024, 5632]), ("w_ffn_out", [2, 2816, 1024]), ("ropeC", [64, NT]), ("ropeS", [64, NT]),
                 ("qaug", [16, NT]), ("kaug", [16, NKEY]), ("kcT", [2, 128, 512]), ("vc", [2, 512, 128]),
                 ("ident", [128, 128]), ("maskL", [128, 128]), ("maskU", [128, 128]), ("psw", [64, 64])):
        kb.din(n, s)
    kb.dout("yT", [1024, NT])
    kb.dout("kT_out", [2, 128, NT])
    kb.dout("v_out", [2, NT, 128])
    kb.dout("st_out", [2, 128, 512])
    kb.dscr("MGd", [2, 128, 4096], BF16, ext=True)
    kb.dscr("WBd", [2, 2, 128, 4096], BF16, ext=True)
    kb.dscr("QDd", [2, 2, 128, 4096], BF16, ext=True)
    kb.dscr("SMd", [2, 128, NSM], F32, ext=True)
    kb.dscr("u_scr", [512, NT], BF16)
    kb.dscr("y_scr", [32, 128, 256], BF16)
    kb.dscr("hb_scr", [4, 128, 4096], BF16)
    dbgc = [0]

    def dbg(name, ap, shape, dt, reads, parts=128):
        if not DEBUG:
            return
        t = kb.nc.dram_tensor("dbg_" + name, [parts] + list(shape), dt, kind="ExternalOutput").ap()
        kb.dma("sp", t, ap, reads, [("dbg", name)], "dbg")

    ident = AR.alloc([128], F32)
    ones_b = AR.alloc([128], BF16)
    psw_b = AR.alloc([64], BF16, parts=64)
    flag = AR.alloc([1], F32)
    halfpi = AR.alloc([1], F32)
    epsT = AR.alloc([1], F32)
    cond = AR.alloc([8], F32)
    condS = AR.alloc([8], BF16)
    MODS = AR.alloc([2, 48], F32)
    DER = AR.alloc([2, 6, 8], F32)
    n1 = AR.alloc([2, 8], F32)
    n2 = AR.alloc([2, 8], F32)
    fnorm = AR.alloc([8], F32)
    qkg = AR.alloc([2, 2], F32, parts=64)
    bglu = AR.alloc([2, 4], F32)
    bmod = AR.alloc([2, 48], F32)
    dSt = AR.alloc([2, 32], F32)
    kb.dma("sp", ident, d["ident"], [], ["ident"], "c0")
    kb.dma("sp", flag, d["flag"], [], ["flag"], "c0")
    kb.dma("sp", cond, d["cond"], [], ["cond"], "c0")
    kb.dma("sp", n1, d["norm1T"].rearrange("l p k -> p l k"), [], ["n1"], "c0")
    kb.dma("sp", n2, d["norm2T"].rearrange("l p k -> p l k"), [], ["n2"], "c0")
    kb.dma("sp", fnorm, d["fnormT"], [], ["fnorm"], "c0")
    kb.dma("sp", qkg, d["qkg"].rearrange("l p k -> p l k"), [], ["qkg"], "c0")
    kb.dma("sp", bglu, d["b_gluT"].rearrange("l p k -> p l k"), [], ["bglu"], "c0")
    kb.dma("sp", bmod, d["b_modT"].rearrange("l p k -> p l k"), [], ["bmod"], "c0")
    kb.dma("sp", dSt, d["dS"].rearrange("l p k -> p l k"), [], ["dSt"], "c0")
    kb.dma("pool", psw_b, d["psw"], [], ["psw_b"], "c1")
    kb.ms("dve", ones_b, 1.0, ["ones_b"])
    kb.ms("dve", halfpi, float(np.pi / 2), ["halfpi"])
    kb.ms("dve", epsT, EPS, ["epsT"])
    kb.act(condS, cond, AF.Silu, ["cond"], ["condS"])
    xoff = AR.mark()

    def mods_pieces(l, WM, pw=512):
        wv = d["w_mod"][l].rearrange("(k p) n -> p k n", p=128)
        pieces = []
        mt = pw // 128
        for i in range(6144 // pw):
            def dma_fn(i=i):
                kb.dma("pool", WM[i % 2], wv[:, :, i * pw:(i + 1) * pw], [], [("WM", i % 2)], ("WM", i % 2))

            def mm_fn(i=i):
                for m in range(mt):
                    col = i * mt + m
                    for k in range(8):
                        kb.mm(ps[:, 0, col:col + 1], WM[i % 2][:, k, m * 128:(m + 1) * 128], condS[:, k:k + 1],
                              k == 0, k == 7, [("WM", i % 2), "condS"], [("ps", 0)])
                kb.cp("dve", MODS[:, l, i * mt:(i + 1) * mt], ps[:, 0, i * mt:(i + 1) * mt], [("ps", 0)], [("MODS", l)])
            pieces.append((dma_fn, mm_fn))

        def fin():
            kb.tt("dve", MODS[:, l, :], MODS[:, l, :], bmod[:, l, :], ADD, [("MODS", l), "bmod"], [("MODS", l)])
            nn = [n1, n2]
            for w in range(2):
                kb.stt(DER[:, l, 3 * w + 0, :], MODS[:, l, 24 * w + 8:24 * w + 16], 1.0, nn[w][:, l, :], ADD, MUL,
                       [("MODS", l), "n1", "n2"], [("DER", l)])
                kb.cp("dve", DER[:, l, 3 * w + 1, :], MODS[:, l, 24 * w:24 * w + 8], [("MODS", l)], [("DER", l)])
                kb.cp("dve", DER[:, l, 3 * w + 2, :], MODS[:, l, 24 * w + 16:24 * w + 24], [("MODS", l)], [("DER", l)])
        return pieces, fin

    def ssm_pre(l):
        mk = AR.mark()
        A = AR.alloc
        maskL = A([128], F32)
        maskU = A([128], F32)
        kb.dma("sp", maskL, d["maskL"], [], ["maskL"], "pre")
        kb.dma("sp", maskU, d["maskU"], [], ["maskU"], "pre")
        are, aim, ldt = A([2, 16], F32), A([2, 16], F32), A([2, 16], F32)
        kb.dma("sp", are, d["a_reT"][l].rearrange("d p g -> p d g"), [], ["are"], "pre")
        kb.dma("sp", aim, d["a_imT"][l].rearrange("d p g -> p d g"), [], ["aim"], "pre")
        kb.dma("sp", ldt, d["ldtT"][l].rearrange("d p g -> p d g"), [], ["ldt"], "pre")
        br, bi, cr, ci = (A([2, 16, 16], F32) for _ in range(4))
        for t, n in ((br, "b_reT"), (bi, "b_imT"), (cr, "c_reT"), (ci, "c_imT")):
            kb.dma("sp", t, d[n][l].rearrange("d p g h -> p d g h"), [], [n], "pre")
        SM = A([NSM], F32)
        LV = SM[:, 0:480].rearrange("p (g d k j) -> p g d k j", g=16, d=2, k=5)
        FX = SM[:, 480:2528].rearrange("p (g d r i) -> p g d r i", g=16, d=2, r=2)
        CBm = SM[:, 2528:2816].rearrange("p (g d k j) -> p g d k j", g=16, d=2, k=3)
        EBH0 = SM[:, 2816:2880].rearrange("p (g d r) -> p g d r", g=16, d=2)
        H0 = SM[:, 2880:2944].rearrange("p (g d r) -> p g d r", g=16, d=2)
        kb.dma("sp", SM[:, 2880:2944], d["h0T"][l], [], ["H0"], "pre")

        cnt = [0]

        def small(name):
            cnt[0] += 1
            return A([2, 16], F32), "%s_%d_%d" % (name, l, cnt[0])

        def V(eng, out, a, b, op):
            kb.tt(eng, out[0], a[0], b[0], op, [a[1], b[1]], [out[1]])

        dt_ = small("dt")
        kb.act(dt_[0], ldt, AF.Exp, ["ldt"], [dt_[1]])
        e_ = small("e")
        ang = small("ang")
        V("dve", e_, (are, "are"), dt_, MUL)
        V("dve", ang, (aim, "aim"), dt_, MUL)
        mag, magn, c16, s16 = small("mag"), small("magn"), small("c16"), small("s16")
        kb.act(mag[0], e_[0], AF.Exp, [e_[1]], [mag[1]], scale=1.0 / 16)
        kb.act(magn[0], e_[0], AF.Exp, [e_[1]], [magn[1]], scale=-1.0 / 16)
        kb.act(c16[0], ang[0], AF.Sin, [ang[1], "halfpi"], [c16[1]], bias=halfpi, scale=1.0 / 16)
        kb.act(s16[0], ang[0], AF.Sin, [ang[1]], [s16[1]], scale=1.0 / 16)
        E = {}

        def cmul(a, b):
            (ar_, ai_), (br_, bi_) = a, b
            t1, t2, rr, ii = small("t1"), small("t2"), small("rr"), small("ii")
            t3, t4 = small("t3"), small("t4")
            V("dve", t1, ar_, br_, MUL)
            V("dve", t2, ai_, bi_, MUL)
            V("dve", t3, ar_, bi_, MUL)
            V("dve", t4, ai_, br_, MUL)
            V("dve", rr, t1, t2, SUB)
            V("dve", ii, t3, t4, ADD)
            return (rr, ii)

        mu_r, mu_i, nu_r, nu_i = small("mur"), small("mui"), small("nur"), small("nui")
        V("dve", mu_r, mag, c16, MUL)
        V("dve", mu_i, mag, s16, MUL)
        V("dve", nu_r, magn, c16, MUL)
        kb.stt(nu_i[0], magn[0], -1.0, s16[0], MUL, MUL, [magn[1], s16[1]], [nu_i[1]])
        cur, curn = (mu_r, mu_i), (nu_r, nu_i)
        for _ in range(4):
            cur = cmul(cur, cur)
            curn = cmul(curn, curn)
        E[1], E[-1] = cur, curn
        for j in range(2, 9):
            E[j] = cmul(E[j - 1], E[1])
        for j in range(2, 8):
            E[-j] = cmul(E[-j + 1], E[-1])
        E[0] = (small("one"), small("zero"))
        kb.ms("dve", E[0][0][0], 1.0, [E[0][0][1]])
        kb.ms("dve", E[0][1][0], 0.0, [E[0][1][1]])
        j = 8
        while j < 1024:
            E[2 * j] = cmul(E[j], E[j])
            j *= 2
        t1, t2, den, rden, lm1 = small("za"), small("zb"), small("den"), small("rden"), small("lm1")
        V("dve", t1, (are, "are"), (are, "are"), MUL)
        V("dve", t2, (aim, "aim"), (aim, "aim"), MUL)
        V("dve", den, t1, t2, ADD)
        kb.rcp(rden[0], den[0], [den[1]], [rden[1]])
        kb.ts("dve", lm1[0], E[1][0][0], -1.0, ADD, [E[1][0][1]], [lm1[1]])
        t3, t4, t5, t6, zr, zi = small("zc"), small("zd"), small("ze"), small("zf"), small("zr"), small("zi")
        V("dve", t3, lm1, (are, "are"), MUL)
        V("dve", t4, E[1][1], (aim, "aim"), MUL)
        V("dve", t5, t3, t4, ADD)
        V("dve", zr, t5, rden, MUL)
        V("dve", t3, E[1][1], (are, "are"), MUL)
        V("dve", t4, lm1, (aim, "aim"), MUL)
        V("dve", t6, t3, t4, SUB)
        V("dve", zi, t6, rden, MUL)
        bbr, bbi, w1, w2 = (A([2, 16, 16], F32) for _ in range(4))
        sh = [128, 2, 16, 16]
        zrb, zib = bc(zr[0], sh, 3), bc(zi[0], sh, 3)
        kb.tt("dve", w1, br, zrb, MUL, ["b_reT", zr[1]], ["w1"])
        kb.tt("dve", w2, bi, zib, MUL, ["b_imT", zi[1]], ["w2"])
        kb.tt("dve", bbr, w1, w2, SUB, ["w1", "w2"], ["bbr"])
        kb.tt("dve", w1, bi, zrb, MUL, ["b_imT", zr[1]], ["w1"])
        kb.tt("dve", w2, br, zib, MUL, ["b_reT", zi[1]], ["w2"])
        kb.tt("dve", bbi, w1, w2, ADD, ["w1", "w2"], ["bbi"])

        slots = [(A([16, 8, 16], F32), A([16, 8, 16], F32)) for _ in range(4)]
        tmp = {"dve": [A([16, 16], F32) for _ in range(4)], "pool": [A([16, 16], F32) for _ in range(4)]}

        def table(eng, slot, dirn, Vr, Vi, vnames, expo, neg):
            Tr, Ti = slots[slot]
            tn = ("slot", slot)
            s3 = [128, 16, 16]
            for pos in range(8):
                q1, q2, q3, q4 = tmp[eng]
                qn1, qn2, qn3, qn4 = ("tq%d%s" % (i_, eng) for i_ in range(4))
                tnp = ("slot", slot, pos)
                er, ei = E[expo[pos]]
                erb, eib = bc(er[0][:, dirn, :], s3, 2), bc(ei[0][:, dirn, :], s3, 2)
                vr, vi = Vr[:, dirn], Vi[:, dirn]
                kb.tt(eng, q1, vr, erb, MUL, [vnames[0], er[1]], [qn1])
                kb.tt(eng, q2, vi, eib, MUL, [vnames[1], ei[1]], [qn2])
                kb.tt(eng, q3, vi, erb, MUL, [vnames[1], er[1]], [qn3])
                kb.tt(eng, q4, vr, eib, MUL, [vnames[0], ei[1]], [qn4])
                kb.tt(eng, Tr[:, :, pos, :], q1, q2, SUB, [qn1, qn2], [tnp])
                if not neg:
                    kb.tt(eng, Ti[:, :, pos, :], q3, q4, ADD, [qn3, qn4], [tnp])
                else:
                    kb.tt(eng, q3, q3, q4, ADD, [qn3, qn4], [qn3])
                    kb.ts(eng, Ti[:, :, pos, :], q3, -1.0, MUL, [qn3], [tnp])

        def flat(t, rows, gp):
            return t[rows, gp].rearrange("p s h -> p (s h)")

        MGacc = A([32, 128], F32)
        MG = A([32, 128], BF16)
        Wst = A([16, 2, 128], BF16)
        Qst = A([16, 2, 128], BF16)
        tmpT = A([4, 128], F32)
        BB, CC = ("bbr", "bbi"), ("c_reT", "c_imT")
        rng8 = list(range(8))
        table("dve", 0, 0, bbr, bbi, BB, [-s for s in rng8], False)
        table("dve", 1, 0, cr, ci, CC, rng8, True)
        table("pool", 2, 1, bbr, bbi, BB, rng8, False)
        table("pool", 3, 1, cr, ci, CC, [-t for t in rng8], True)

        tb16 = [A([16, 8, 16], BF16) for _ in range(4)]

        def toeplitz(sa, sb, first):
            for i_, (sl, ri_) in enumerate(((sa, 0), (sa, 1), (sb, 0), (sb, 1))):
                kb.act(tb16[i_], slots[sl][ri_], AF.Copy, [("slot", sl, p_) for p_ in range(8)], [("tb16", i_)])
            for gq in range(8):
                bank = gq % 2
                for q in range(4):
                    g = 4 * gq + q
                    gp, g2 = g // 2, g % 2
                    rows = slice(64 * g2, 64 * g2 + 64)
                    o = ps[:, bank, q * 128:(q + 1) * 128]
                    mmf = kb.mm if g2 == 0 else kb.mm_hi
                    mmf(o, flat(tb16[0], rows, gp), flat(tb16[2], rows, gp), True, False,
                        [("tb16", 0), ("tb16", 2)], [("ps", bank)])
                    mmf(o, flat(tb16[1], rows, gp), flat(tb16[3], rows, gp), False, True,
                        [("tb16", 1), ("tb16", 3)], [("ps", bank)])
                pv = ps[:, bank, :].rearrange("p (q n) -> p q n", q=4)
                acc = MGacc[:, 4 * gq:4 * gq + 4, :]
                if first:
                    kb.tt("dve", acc, pv, bc(maskL, [128, 4, 128], 1), MUL, [("ps", bank), "maskL"], [("MGacc", gq)])
                    for q in range(4):
                        g = 4 * gq + q
                        kb.stt(MGacc[:, g, :], ident, dSt[:, l, g:g + 1], MGacc[:, g, :], MUL, ADD,
                               ["ident", "dSt", ("MGacc", gq)], [("MGacc", gq)])
                else:
                    kb.tt("dve", tmpT, pv, bc(maskU, [128, 4, 128], 1), MUL, [("ps", bank), "maskU"], ["tmpT"])
                    kb.tt("dve", MG[:, 4 * gq:4 * gq + 4, :], tmpT, acc, ADD, ["tmpT", ("MGacc", gq)], ["MG"])

        def wb_out(slot, dirn):
            for gq in range(8):
                bank = 2 + gq % 2
                for q in range(4):
                    gp, ri = 2 * gq + q // 2, q % 2
                    kb.tr(ps[:, bank, q * 128:(q + 1) * 128], flat(slots[slot][ri], slice(0, 128), gp), ident,
                          [("slot", slot, p_) for p_ in range(8)] + ["ident"], [("ps", bank)])
                kb.act(Wst[:, 2 * gq:2 * gq + 2, :, :], ps[:, bank, :].rearrange("p (g r n) -> p g r n", g=2, r=2),
                       AF.Copy, [("ps", bank)], ["Wst"])
            kb.dma("sp", d["WBd"][l, dirn], Wst.rearrange("p g r n -> p (g r n)"), ["Wst"], [("WBd", l, dirn)], "pre")

        def qd_out(slot, dirn):
            for ri in range(2):
                kb.cp("dve", Qst[:, :, ri, :], slots[slot][ri].rearrange("p g s h -> p g (s h)"),
                      [("slot", slot, p_) for p_ in range(8)], ["Qst"])
            kb.dma("sp", d["QDd"][l, dirn], Qst.rearrange("p g r n -> p (g r n)"), ["Qst"], [("QDd", l, dirn)], "pre")

        toeplitz(0, 1, True)
        table("dve", 0, 0, bbr, bbi, BB, [7 - s for s in rng8], False)
        wb_out(0, 0)
        table("dve", 1, 0, cr, ci, CC, [t + 1 for t in rng8], True)
        qd_out(1, 0)
        toeplitz(2, 3, False)
        wb_out(2, 1)
        table("pool", 3, 1, cr, ci, CC, [8 - t for t in rng8], True)
        qd_out(3, 1)
        kb.dma("sp", d["MGd"][l], MG.rearrange("p g n -> p (g n)"), ["MG"], [("MGd", l)], "pre")

        def perm(ap2):
            return ap2.rearrange("p d g -> p g d")
        for k in range(5):
            er, ei = E[8 * (2 ** k)]
            kb.cp("dve", LV[:, :, :, k, 0], perm(er[0]), [er[1]], ["SM"])
            kb.cp("dve", LV[:, :, :, k, 1], perm(ei[0]), [ei[1]], ["SM"])
            kb.ts("dve", LV[:, :, :, k, 2], perm(ei[0]), -1.0, MUL, [ei[1]], ["SM"])
        for k in range(3):
            er, ei = E[256 * (2 ** k)]
            kb.cp("dve", CBm[:, :, :, k, 0], perm(er[0]), [er[1]], ["SM"])
            kb.cp("dve", CBm[:, :, :, k, 1], perm(ei[0]), [ei[1]], ["SM"])
            kb.ts("dve", CBm[:, :, :, k, 2], perm(ei[0]), -1.0, MUL, [ei[1]], ["SM"])
        f1, f2 = A([16, 16], F32), A([16, 16], F32)

        def fx_mul(dirn, dst, src, n, ej):
            er, ei = E[ej]
            s3 = [128, 16, n]
            erb, eib = bc(er[0][:, dirn, :], s3, 2), bc(ei[0][:, dirn, :], s3, 2)
            sr, si = FX[:, :, dirn, 0, src[0]:src[1]], FX[:, :, dirn, 1, src[0]:src[1]]
            dr, di = FX[:, :, dirn, 0, dst[0]:dst[1]], FX[:, :, dirn, 1, dst[0]:dst[1]]
            a1, a2 = f1[:, :, 0:n], f2[:, :, 0:n]
            kb.tt("dve", a1, sr, erb, MUL, ["SM", er[1]], ["f1"])
            kb.tt("dve", a2, si, eib, MUL, ["SM", ei[1]], ["f2"])
            kb.tt("dve", dr, a1, a2, SUB, ["f1", "f2"], ["SM"])
            kb.tt("dve", a1, si, erb, MUL, ["SM", er[1]], ["f1"])
            kb.tt("dve", a2, sr, eib, MUL, ["SM", ei[1]], ["f2"])
            kb.tt("dve", di, a1, a2, ADD, ["f1", "f2"], ["SM"])
        for ri in range(2):
            kb.cp("dve", FX[:, :, 0, ri, 0], E[8][ri][0][:, 0, :], [E[8][ri][1]], ["SM"])
            kb.cp("dve", FX[:, :, 0, ri, 1], E[16][ri][0][:, 0, :], [E[16][ri][1]], ["SM"])
            kb.cp("dve", FX[:, :, 1, ri, 31], E[8][ri][0][:, 1, :], [E[8][ri][1]], ["SM"])
            kb.cp("dve", FX[:, :, 1, ri, 30], E[16][ri][0][:, 1, :], [E[16][ri][1]], ["SM"])
        n_ = 2
        while n_ < 32:
            fx_mul(0, (n_, 2 * n_), (0, n_), n_, 8 * n_)
            fx_mul(1, (32 - 2 * n_, 32 - n_), (32 - n_, 32), n_, 8 * n_)
            n_ *= 2
        er, ei = E[256]
        g1, g2, g3 = small("g1"), small("g2"), small("g3")
        pg = lambda x: x.rearrange("p d g -> p g d")
        h0r, h0i = H0[:, :, :, 0], H0[:, :, :, 1]
        g1p, g2p = pg(g1[0]), pg(g2[0])
        kb.tt("dve", g1p, h0r, pg(er[0]), MUL, ["H0", er[1]], [g1[1]])
        kb.tt("dve", g2p, h0i, pg(ei[0]), MUL, ["H0", ei[1]], [g2[1]])
        kb.tt("dve", EBH0[:, :, :, 0], g1p, g2p, SUB, [g1[1], g2[1]], ["SM"])
        kb.tt("dve", g1p, h0i, pg(er[0]), MUL, ["H0", er[1]], [g1[1]])
        kb.tt("dve", g2p, h0r, pg(ei[0]), MUL, ["H0", ei[1]], [g2[1]])
        kb.tt("dve", EBH0[:, :, :, 1], g1p, g2p, ADD, [g1[1], g2[1]], ["SM"])
        kb.dma("sp", d["SMd"][l], SM, ["SM", "H0"], [("SMd", l)], "pre")
        P.barrier()
        AR.reset(mk)

    WM0 = [AR.alloc([8, 512], BF16), AR.alloc([8, 512], BF16)]
    pieces, fin = mods_pieces(0, WM0)
    for dma_fn, mm_fn in pieces:
        dma_fn()
        mm_fn()
    fin()
    dbg("mods0", MODS[:, 0, :], [48], F32, [("MODS", 0)])
    dbg("der0", DER[:, 0], [6, 8], F32, [("DER", 0)])
    for l in range(DEPTH):
        ssm_pre(l)
    P.barrier()
    AR.reset(xoff)

    X = AR.alloc([8, NT], F32)
    GL = AR.alloc([4, NT], BF16)
    kb.dma("sp", X, d["xT"].rearrange("(k p) n -> p k n", p=128), [], [("X", k, b) for k in range(8) for b in range(4)], "X")
    base = AR.mark()
    Xn = lambda k, b: ("X", k, b)
    allX = lambda b: [Xn(k, b) for k in range(8)]

    def norm_block(l, w, blk, HB, hbname, T):
        cs = slice(blk * 512, blk * 512 + 512)
        for k in range(8):
            sq = T["sq"][k % 2]
            kb.act(sq, X[:, k, cs], AF.Square, [Xn(k, blk)], [("sq", k % 2)])
            kb.mm(ps[:, 0, :], ones_b, sq, k == 0, k == 7, ["ones_b", ("sq", k % 2)], [("ps", 0)])
        kb.act(T["lnv"], ps[:, 0, :], AF.Ln, [("ps", 0), "epsT"], ["lnv"], bias=epsT, scale=1.0 / D_MODEL)
        kb.act(T["rstd"], T["lnv"], AF.Exp, ["lnv"], ["rstd"], scale=-0.5)
        for k in range(8):
            tn = T["tn"][k % 2]
            kb.stt(tn, X[:, k, cs], DER[:, l, 3 * w, k:k + 1], T["rstd"], MUL, MUL,
                   [Xn(k, blk), ("DER", l), "rstd"], [("tn", k % 2)])
            kb.act(HB[:, k, :], tn, AF.Identity, [("tn", k % 2), ("DER", l)], [(hbname, k)],
                   bias=DER[:, l, 3 * w + 1, k:k + 1])

    def norm_tmps():
        return {"sq": [AR.alloc([512], BF16), AR.alloc([512], BF16)], "lnv": AR.alloc([512], F32),
                "rstd": AR.alloc([512], F32), "tn": [AR.alloc([512], F32), AR.alloc([512], F32)]}

    def proj(HB, hbname, W, wname, c0, M, out, bank):
        for k in range(8):
            kb.mm(out, W[:, k, c0:c0 + M], HB[:, k, :], k == 0, k == 7, [(hbname, k), wname], [("ps", bank)])

    def phase_ssm(l):
        mk = AR.mark()
        A = AR.alloc
        T = norm_tmps()
        HB = A([8, 512], BF16)
        WU = A([8, 512], BF16)
        UST = A([4, 512], BF16)
        UG = A([32, 256], BF16)
        mk2 = AR.mark()
        MG = A([32, 128], BF16)
        WB = A([2, 16, 2, 128], BF16)
        QD = A([2, 16, 2, 128], BF16)
        SM = A([NSM], F32)
        LV = SM[:, 0:480].rearrange("p (g d k j) -> p g d k j", g=16, d=2, k=5)
        FX = SM[:, 480:2528].rearrange("p (g d r i) -> p g d r i", g=16, d=2, r=2)
        CBm = SM[:, 2528:2816].rearrange("p (g d k j) -> p g d k j", g=16, d=2, k=3)
        EBH0 = SM[:, 2816:2880].rearrange("p (g d r) -> p g d r", g=16, d=2)
        H0 = SM[:, 2880:2944].rearrange("p (g d r) -> p g d r", g=16, d=2)
        SAB = [A([2, 2, 8, 64], F32), A([2, 2, 8, 64], F32)]
        CAB = [A([2, 2, 16], F32), A([2, 2, 16], F32)]
        CIN = A([2, 2, 8], F32)
        FT = [A([8, 32], F32), A([8, 32], F32)]
        HP = [A([2, 2, 8, 32], BF16), A([2, 2, 8, 32], BF16)]
        ST = A([16, 2, 2, 8], F32)
        YS = [A([256], BF16) for _ in range(4)]
        kb.dma("pool", WU, d["w_in"][l].rearrange("(k p) n -> p k n", p=128)[:, :, 768:1280], [], ["WU"], "WU")
        kb.dma("sp", MG, d["MGd"][l].rearrange("p (g n) -> p g n", g=32), [("MGd", l)], ["MG"], "pk")
        for dd in range(2):
            kb.dma("sp", WB[:, dd], d["WBd"][l, dd].rearrange("p (g r n) -> p g r n", g=16, r=2), [("WBd", l, dd)], [("WB", dd)], "pk")
            kb.dma("sp", QD[:, dd], d["QDd"][l, dd].rearrange("p (g r n) -> p g r n", g=16, r=2), [("QDd", l, dd)], [("QD", dd)], "pk")
        kb.dma("sp", SM, d["SMd"][l], [("SMd", l)], ["SM"], "pk")
        for t_, tn_ in zip(SAB, ("SA", "SB")):
            kb.ms("dve", t_, 0.0, [(tn_, a_, b_) for a_ in range(2) for b_ in range(2)])
        for t_ in CAB:
            kb.ms("dve", t_, 0.0, [("CA", a_, b_) for a_ in range(2) for b_ in range(2)] + [("CB", a_, b_) for a_ in range(2) for b_ in range(2)])
        for blk in range(4):
            norm_block(l, 0, blk, HB, "HB", T)
            kb.dma("sp", d["hb_scr"][blk], HB.rearrange("p k n -> p (k n)"), [("HB", k) for k in range(8)],
                   [("hb_scr", blk)], "hbw")
            if l == 0 and blk == 0:
                dbg("hb", HB, [8, 512], BF16, [("HB", k) for k in range(8)])
            for m in range(4):
                bank = 1 + m % 2
                proj(HB, "HB", WU, "WU", m * 128, 128, ps[:, bank, :], bank)
                kb.act(UST[:, m, :], ps[:, bank, :], AF.Copy, [("ps", bank)], [("UST", m)])
                kb.dma("sp", d["u_scr"][m * 128:(m + 1) * 128, blk * 512:(blk + 1) * 512], UST[:, m, :],
                       [("UST", m)], [("u_scr", blk)], "us")
            for tl in range(2):
                tau = 2 * blk + tl
                src = d["u_scr"][:, tau * 256:(tau + 1) * 256].rearrange("(g h) c -> h g c", h=16)
                kb.dma("sp", UG[16 * tau:16 * tau + 16, :, :], src, [("u_scr", blk)], ["UG"], "ug")
        if l == 0:
            dbg("ug", UG, [32, 256], BF16, ["UG"])
        for gp in range(16):
            for dd in range(2):
                bank = 5 + dd
                for ri in range(2):
                    for g2 in range(2):
                        kb.mm(ps[64 * g2:64 * g2 + 64, bank, ri * 256:(ri + 1) * 256],
                              WB[:, dd, gp, ri, 64 * g2:64 * g2 + 64], UG[:, 2 * gp + g2, :], True, True,
                              [("WB", dd), "UG"], [("ps", bank)])
                kb.act(SAB[0][:, dd, :, :, 16:48], ps[:, bank, :].rearrange("p (r b i) -> p r b i", r=2, b=8),
                       AF.Copy, [("ps", bank)], [("SA", dd, 0), ("SA", dd, 1)])
            for k in range(5):
                sh = 2 ** k
                src, dst = SAB[k % 2], SAB[(k + 1) % 2]
                sn, dn = ("SA", "SB")[k % 2], ("SA", "SB")[(k + 1) % 2]
                for step in range(4):
                    for dd in range(2):
                        lo, hi = (16 - sh, 48 - sh) if dd == 0 else (16 + sh, 48 + sh)
                        er, ei, nei = (LV[:, gp, dd, k, j:j + 1] for j in range(3))
                        sr, si = src[:, dd, 0], src[:, dd, 1]
                        dr, di = dst[:, dd, 0, :, 16:48], dst[:, dd, 1, :, 16:48]
                        dnr, dni = (dn, dd, 0), (dn, dd, 1)
                        srd = [(sn, dd, 0), (sn, dd, 1), "SM"]
                        if step == 0:
                            kb.stt(dr, sr[:, :, lo:hi], er, sr[:, :, 16:48], MUL, ADD, srd, [dnr])
                        elif step == 1:
                            kb.stt(di, si[:, :, lo:hi], er, si[:, :, 16:48], MUL, ADD, srd, [dni])
                        elif step == 2:
                            kb.stt(dr, si[:, :, lo:hi], nei, dr, MUL, ADD, srd + [dnr], [dnr])
                        else:
                            kb.stt(di, sr[:, :, lo:hi], ei, di, MUL, ADD, srd + [dni], [dni])
            Hl = SAB[1]
            kb.cp("dve", CAB[0][:, 0, :, 4:12], Hl[:, 0, :, :, 47], [("SB", a_, b_) for a_ in range(2) for b_ in range(2)], [("CA", a_, b_) for a_ in range(2) for b_ in range(2)])
            kb.cp("dve", CAB[0][:, 1, :, 4:12], Hl[:, 1, :, :, 16], [("SB", a_, b_) for a_ in range(2) for b_ in range(2)], [("CA", a_, b_) for a_ in range(2) for b_ in range(2)])
            kb.tt("dve", CAB[0][:, 0, :, 4], CAB[0][:, 0, :, 4], EBH0[:, gp, 0, :], ADD, [("CA", a_, b_) for a_ in range(2) for b_ in range(2)] + ["SM"], [("CA", a_, b_) for a_ in range(2) for b_ in range(2)])
            kb.tt("dve", CAB[0][:, 1, :, 11], CAB[0][:, 1, :, 11], EBH0[:, gp, 1, :], ADD, [("CA", a_, b_) for a_ in range(2) for b_ in range(2)] + ["SM"], [("CA", a_, b_) for a_ in range(2) for b_ in range(2)])
            for k in range(3):
                sh = 2 ** k
                src, dst = CAB[k % 2], CAB[(k + 1) % 2]
                sn, dn = ("CA", "CB")[k % 2], ("CA", "CB")[(k + 1) % 2]
                sn4 = [(sn, a_, b_) for a_ in range(2) for b_ in range(2)]
                for step in range(4):
                    for dd in range(2):
                        lo, hi = (4 - sh, 12 - sh) if dd == 0 else (4 + sh, 12 + sh)
                        er, ei, nei = (CBm[:, gp, dd, k, j:j + 1] for j in range(3))
                        sr, si = src[:, dd, 0], src[:, dd, 1]
                        dr, di = dst[:, dd, 0, 4:12], dst[:, dd, 1, 4:12]
                        dnr, dni = (dn, dd, 0), (dn, dd, 1)
                        if step == 0:
                            kb.stt(dr, sr[:, lo:hi], er, sr[:, 4:12], MUL, ADD, sn4 + ["SM"], [dnr])
                        elif step == 1:
                            kb.stt(di, si[:, lo:hi], er, si[:, 4:12], MUL, ADD, sn4 + ["SM"], [dni])
                        elif step == 2:
                            kb.stt(dr, si[:, lo:hi], nei, dr, MUL, ADD, sn4 + ["SM", dnr], [dnr])
                        else:
                            kb.stt(di, sr[:, lo:hi], ei, di, MUL, ADD, sn4 + ["SM", dni], [dni])
            Il = CAB[1]
            kb.ts("dve", CIN[:, 0, :, 1:8], Il[:, 0, :, 4:11], flag, MUL, [("CB", a_, b_) for a_ in range(2) for b_ in range(2)] + ["flag"], ["CIN"])
            kb.cp("dve", CIN[:, 0, :, 0], H0[:, gp, 0, :], ["SM"], ["CIN"])
            kb.ts("dve", CIN[:, 1, :, 0:7], Il[:, 1, :, 5:12], flag, MUL, [("CB", a_, b_) for a_ in range(2) for b_ in range(2)] + ["flag"], ["CIN"])
            kb.cp("dve", CIN[:, 1, :, 7], H0[:, gp, 1, :], ["SM"], ["CIN"])
            s3 = [128, 8, 32]
            for dd in range(2):
                fr, fi = bc(FX[:, gp, dd, 0, :], s3, 1), bc(FX[:, gp, dd, 1, :], s3, 1)
                cr_, ci_ = bc(CIN[:, dd, 0, :], s3, 2), bc(CIN[:, dd, 1, :], s3, 2)
                hr, hi_ = Hl[:, dd, 0, :, 16:48], Hl[:, dd, 1, :, 16:48]
                for ix, (fa, ca, tgt, op, rix) in enumerate(((fr, cr_, hr, ADD, 0), (fr, ci_, hi_, ADD, 1),
                                                             (fi, ci_, hr, SUB, 0), (fi, cr_, hi_, ADD, 1))):
                    ft = FT[ix % 2]
                    kb.tt("dve", ft, fa, ca, MUL, ["SM", "CIN"], [("FT", ix % 2)])
                    kb.tt("dve", tgt, tgt, ft, op, [("SB", dd, rix), ("FT", ix % 2)], [("SB", dd, rix)])
            hp = HP[gp % 2]
            hn = ("HP", gp % 2)
            kb.cp("dve", hp[:, 0, :, :, 1:32], Hl[:, 0, :, :, 16:47], [("SB", a_, b_) for a_ in range(2) for b_ in range(2)], [hn])
            kb.cp("dve", hp[:, 0, :, :, 0], CIN[:, 0, :, :], ["CIN"], [hn])
            kb.cp("dve", hp[:, 1, :, :, 0:31], Hl[:, 1, :, :, 17:48], [("SB", a_, b_) for a_ in range(2) for b_ in range(2)], [hn])
            kb.cp("dve", hp[:, 1, :, :, 31], CIN[:, 1, :, :], ["CIN"], [hn])
            kb.cp("dve", ST[:, gp, 0, :, :], Hl[:, 0, :, :, 47], [("SB", a_, b_) for a_ in range(2) for b_ in range(2)], ["ST"])
            kb.cp("dve", ST[:, gp, 1, :, :], Hl[:, 1, :, :, 16], [("SB", a_, b_) for a_ in range(2) for b_ in range(2)], ["ST"])
            for g2 in range(2):
                g = 2 * gp + g2
                bank = 7 if g2 == 0 else 4
                rows = slice(64 * g2, 64 * g2 + 64)
                o = ps[:, bank, 0:256]
                kb.mm(o, MG[:, g, :], UG[:, g, :], True, False, ["MG", "UG"], [("ps", bank)])
                for dd in range(2):
                    for comp in range(2):
                        last = (dd == 1 and comp == 1)
                        lhsT = QD[rows, dd, gp, comp, :]
                        rhs = hp[rows, dd, comp].rearrange("p b i -> p (b i)")
                        if g2 == 0:
                            kb.mm(o, lhsT, rhs, False, last, [("QD", dd), hn], [("ps", bank)])
                        else:
                            kb.mm_hi(o, lhsT, rhs, False, last, [("QD", dd), hn], [("ps", bank)])
                ys = YS[g % 4]
                kb.act(ys, o, AF.Gelu, [("ps", bank)], [("YS", g % 4)])
                kb.dma("sp", d["y_scr"][g], ys, [("YS", g % 4)], [("y_scr", g)], "ys")
                src = d["y_scr"][g].rearrange("(t h) c -> h t c", h=16)
                dstv = GL[16 * (g % 8):16 * (g % 8) + 16, g // 8, :].rearrange("h (t c) -> h t c", t=8)
                kb.dma("sp", dstv, src, [("y_scr", g)], [("GL", g // 8, b) for b in range(4)], "gl")
        kb.dma("sp", d["st_out"][l], ST.rearrange("p g d r b -> p (g d r b)"), ["ST"], [("st_out", l)], "out")
        if l == 0:
            dbg("gl", GL, [4, NT], BF16, [("GL", k, b) for k in range(4) for b in range(4)])
        P.barrier()
        AR.reset(mk2)
        SIG = A([4, 512], BF16)
        WG = A([4, 512], BF16)
        kb.dma("pool", WG, d["w_glu"][l].rearrange("(k p) n -> p k n", p=128), [], ["WG"], "WG")
        for blk in range(4):
            cs = slice(blk * 512, blk * 512 + 512)
            for m in range(4):
                bank = 1 + m % 2
                for k in range(4):
                    kb.mm(ps[:, bank, :], WG[:, k, m * 128:(m + 1) * 128], GL[:, k, cs], k == 0, k == 3,
                          ["WG", ("GL", k, blk)], [("ps", bank)])
                kb.act(SIG[:, m, :], ps[:, bank, :], AF.Sigmoid, [("ps", bank), "bglu"], [("SIG", m)],
                       bias=bglu[:, l, m:m + 1])
            for m in range(4):
                kb.tt("dve", GL[:, m, cs], GL[:, m, cs], SIG[:, m, :], MUL, [("GL", m, blk), ("SIG", m)], [("GL", m, blk)])
        if l == 0:
            dbg("so", GL, [4, NT], BF16, [("GL", k, b) for k in range(4) for b in range(4)])
        P.barrier()
        AR.reset(mk)

    def phase_att(l, ATT, hook):
        mk = AR.mark()
        A = AR.alloc
        HB = A([8, 512], BF16)
        WQ = A([8, 768], BF16)
        QT = A([8, 512], BF16, parts=80)
        KT = A([2, NKEY], BF16, parts=80)
        VA = A([20, 2, 80], BF16)
        PT = [A([2, 512], BF16) for _ in range(2)]
        sqh = A([512], BF16, parts=64)
        lnh = A([512], F32, parts=64)
        rsh = A([512], F32, parts=64)
        qn = A([512], F32, parts=64)
        qnb = A([512], BF16, parts=64)
        t1 = A([512], F32, parts=64)
        t2 = A([512], F32, parts=64)
        ko = A([512], F32, parts=64)
        RC = A([512], F32, parts=64)
        RS = A([512], F32, parts=64)
        VO2 = A([512], F32)
        numS = VO2[0:64, :]
        VO = VO2[:, 0:128]
        rd = A([2, 512], F32)
        rhl = A([2, 512], BF16)
        wqv = d["w_in"][l].rearrange("(k p) n -> p k n", p=128)
        kb.dma("pool", WQ[:, :, 512:768], wqv[:, :, 512:768], [], ["WQkv"], "WQ")
        kb.dma("pool", WQ[:, :, 0:512], wqv[:, :, 0:512], [], ["WQq"], "WQ")
        for h in range(2):
            kb.dma("pool", KT[0:64, h, NT:NKEY], d["kcT"][l, 64 * h:64 * h + 64, :], [], [("KT", h, 4)], "KTc")
            kb.dma("pool", KT[64:80, h, :], d["kaug"], [], [("KTa", h)], "KTc")
        for h in range(2):
            kb.dma("pool", VA[:, 16:20, h, 0:64], d["vc"][l].rearrange("(t p) (h e) -> p t h e", p=128, h=2)[:, :, h, :], [],
                   [("VAc", h)], "VAc")
        kb.ms("dve", VA[:, :, :, 64:80], 1.0, [("VAo")])

        def head_post(psrc, bank, gcol, blk, is_k, dst, dstname):
            cs = slice(blk * 512, blk * 512 + 512)
            kb.act(sqh, psrc, AF.Square, [("ps", bank)], ["sqh"])
            kb.mm(ps[0:64, 2, :], ones_b[0:64, 0:64], sqh, True, True, ["ones_b", "sqh"], [("ps", 2)])
            kb.act(lnh, ps[0:64, 2, :], AF.Ln, [("ps", 2), "epsT"], ["lnh"], bias=epsT[0:64], scale=1.0 / 64)
            kb.act(rsh, lnh, AF.Exp, ["lnh"], ["rsh"], scale=-0.5)
            kb.stt(qn, psrc, qkg[:, l, gcol:gcol + 1], rsh, MUL, MUL, [("ps", bank), "qkg", "rsh"], ["qn"])
            kb.act(qnb, qn, AF.Copy, ["qn"], ["qnb"])
            kb.mm(ps[0:64, 3, :], psw_b, qnb, True, True, ["psw_b", "qnb"], [("ps", 3)])
            kb.tt("dve", t1, qn, RC, MUL, ["qn", "RC"], ["t1"])
            kb.tt("dve", t2, ps[0:64, 3, :], RS, MUL, [("ps", 3), "RC"], ["t2"])
            if is_k:
                kb.tt("dve", ko, t1, t2, ADD, ["t1", "t2"], ["ko"])
                kb.act(dst, ko, AF.Copy, ["ko"], [dstname])
            else:
                kb.tt("dve", dst, t1, t2, ADD, ["t1", "t2"], [dstname])

        def load_rope(blk):
            cs = slice(blk * 512, blk * 512 + 512)
            kb.dma("sp", RC, d["ropeC"][:, cs], [], ["RC"], "rope")
            kb.dma("sp", RS, d["ropeS"][:, cs], [], ["RC"], "rope")

        for blk in range(4):
            cs = slice(blk * 512, blk * 512 + 512)
            kb.dma("sp", HB.rearrange("p k n -> p (k n)"), d["hb_scr"][blk], [("hb_scr", blk)], [("HB", k) for k in range(8)], "hbr")
            load_rope(blk)
            for h in range(2):
                proj(HB, "HB", WQ, "WQkv", 512 + 64 * h, 64, ps[0:64, 1, :], 1)
                head_post(ps[0:64, 1, :], 1, 1, blk, True, KT[0:64, h, cs], ("KT", h, blk))
                kb.dma("sp", d["kT_out"][l, 64 * h:64 * h + 64, cs], ko, ["ko"], [("kT_out", l, h, blk)], "out")
            for tt_ in range(4):
                kt = 4 * blk + tt_
                for k in range(8):
                    kb.mm(ps[:, 3, 0:128], HB[:, k, tt_ * 128:(tt_ + 1) * 128], WQ[:, k, 640:768], k == 0, k == 7,
                          [("HB", k), "WQkv"], [("ps", 3)])
                kb.act(VO, ps[:, 3, 0:128], AF.Copy, [("ps", 3)], ["numS"])
                kb.cp("dve", VA[:, kt, :, 0:64], VO.rearrange("p (h e) -> p h e", h=2), ["numS"], [("VA", kt)])
                kb.dma("sp", d["v_out"][l, kt * 128:(kt + 1) * 128, :], VO, ["numS"], [("v_out", l, kt)], "out")
        def q_stages(h, blk):
            cs = slice(blk * 512, blk * 512 + 512)
            psrc = ps[0:64, 0, :]

            def s0():
                kb.dma("pool", QT[64:80, h, :], d["qaug"][:, cs], [], [("QTa", h)], "QTa")
                proj(HB, "HB", WQ, "WQq", 64 * h, 64, psrc, 0)
                kb.act(sqh, psrc, AF.Square, [("ps", 0)], ["sqh"])

            def s1():
                kb.mm(ps[0:64, 1, :], ones_b[0:64, 0:64], sqh, True, True, ["ones_b", "sqh"], [("ps", 1)])
                kb.act(lnh, ps[0:64, 1, :], AF.Ln, [("ps", 1), "epsT"], ["lnh"], bias=epsT[0:64], scale=1.0 / 64)
                kb.act(rsh, lnh, AF.Exp, ["lnh"], ["rsh"], scale=-0.5)
                kb.stt(qn, psrc, qkg[:, l, 0:1], rsh, MUL, MUL, [("ps", 0), "qkg", "rsh"], ["qn"])
                kb.act(qnb, qn, AF.Copy, ["qn"], ["qnb"])

            def s2():
                kb.mm(ps[0:64, 1, :], psw_b, qnb, True, True, ["psw_b", "qnb"], [("ps", 1)])
                kb.tt("dve", t1, qn, RC, MUL, ["qn", "RC"], ["t1"])
                kb.tt("dve", t2, ps[0:64, 1, :], RS, MUL, [("ps", 1), "RC"], ["t2"])
                kb.tt("dve", QT[0:64, h, :], t1, t2, ADD, ["t1", "t2"], [("QT", h)])
            return [s0, s1, s2]

        def epilogue1(h, blk):
            kb.act(numS, ps[0:64, 6, :], AF.Copy, [("ps", 6)], ["numS"])
            kb.act(rd[64:65, 0, :], ps[64:65, 6, :], AF.Ln, [("ps", 6)], ["rd0"])
            kb.act(rd[64:65, 1, :], rd[64:65, 0, :], AF.Exp, ["rd0"], ["rd1"], scale=-1.0)
            kb.cp("dve", rhl[64:65, 0, :], rd[64:65, 1, :], ["rd1"], ["rh"])
            kb.tt("dve", rhl[64:65, 1, :], rd[64:65, 1, :], rhl[64:65, 0, :], SUB, ["rd1", "rh"], ["rl"])

        def epilogue2(h, blk):
            cs = slice(blk * 512, blk * 512 + 512)
            kb.mm(ps[0:64, 7, :], ones_b[64:65, 0:64], rhl[64:65, 0, :], True, False, ["ones_b", "rh"], [("ps", 7)])
            kb.mm(ps[0:64, 7, :], ones_b[64:65, 0:64], rhl[64:65, 1, :], False, True, ["ones_b", "rl"], [("ps", 7)])
            kb.tt("dve", ATT[:, h, cs], numS, ps[0:64, 7, :], MUL, ["numS", ("ps", 7)], [("ATT", h, blk)])

        unit = 0

        def load_block(blk):
            kb.dma("sp", HB.rearrange("p k n -> p (k n)"), d["hb_scr"][blk], [("hb_scr", blk)], [("HB", k) for k in range(8)], "hbr")
            load_rope(blk)

        pend_epi = None
        for blk in range(4):
            if blk == 0:
                load_block(0)
                for st_ in q_stages(0, 0):
                    st_()
            for h in range(8):
                kvh = h // 4
                if h < 7:
                    nxt = q_stages(h + 1, blk)
                elif blk < 3:
                    load_block(blk + 1)
                    nxt = q_stages(0, blk + 1)
                else:
                    nxt = []
                sched = {}
                if nxt:
                    sched[1], sched[4], sched[7] = nxt[0], nxt[1], nxt[2]
                kdeps = [("KT", kvh, b) for b in range(5)] + [("KTa", kvh)]
                def qk2(kp):
                    b0 = 2 + 2 * (kp % 2)
                    for j in range(2):
                        kt = 2 * kp + j
                        kb.mm(ps[:, b0 + j, :], KT[0:80, kvh, kt * 128:(kt + 1) * 128], QT[0:80, h, :], True, True,
                              kdeps + [("QT", h), ("QTa", h)], [("ps", b0 + j)])
                    kb.act(PT[kp % 2], ps[:, b0:b0 + 2, :], AF.Exp, [("ps", b0), ("ps", b0 + 1)], [("PT", kp % 2)], scale=0.125)

                qk2(0)
                for kp in range(10):
                    if kp + 1 < 10:
                        qk2(kp + 1)
                    for j in range(2):
                        kt = 2 * kp + j
                        kb.mm(ps[0:65, 6, :], VA[:, kt, kvh, 0:65], PT[kp % 2][:, j, :], kt == 0, kt == 19,
                              [("VA", kt), "VAo", ("VAc", 0), ("VAc", 1), ("PT", kp % 2)], [("ps", 6)])
                    if kp == 2 and pend_epi is not None:
                        epilogue2(*pend_epi)
                        pend_epi = None
                    if kp in sched:
                        sched[kp]()
                epilogue1(h, blk)
                pend_epi = (h, blk)
                hook(unit)
                unit += 1
        epilogue2(*pend_epi)
        P.barrier()
        AR.reset(mk)

    def phase_out(l, ATT):
        mk = AR.mark()
        WOa = AR.alloc([8, 1024], BF16, parts=64)
        WOs = AR.alloc([4, 1024], BF16)
        kb.dma("pool", WOa, d["w_out"][l, 0:512, :].rearrange("(h e) n -> e h n", e=64), [], ["WOa"], "WOa")
        kb.dma("pool", WOs, d["w_out"][l, 512:1024, :].rearrange("(k p) n -> p k n", p=128), [], ["WOs"], "WOs")
        for blk in range(4):
            cs = slice(blk * 512, blk * 512 + 512)
            for m in range(8):
                bank = m % 2
                for h in range(8):
                    kb.mm(ps[:, bank, :], WOa[:, h, m * 128:(m + 1) * 128], ATT[:, h, cs], h == 0, False,
                          ["WOa", ("ATT", h, blk)], [("ps", bank)])
                for k in range(4):
                    kb.mm(ps[:, bank, :], WOs[:, k, m * 128:(m + 1) * 128], GL[:, k, cs], False, k == 3,
                          ["WOs", ("GL", k, blk)], [("ps", bank)])
                kb.stt(X[:, m, cs], ps[:, bank, :], DER[:, l, 2, m:m + 1], X[:, m, cs], MUL, ADD,
                       [("ps", bank), ("DER", l), Xn(m, blk)], [Xn(m, blk)])
        P.barrier()
        AR.reset(mk)

    def phase_ffn(l):
        mk = AR.mark()
        A = AR.alloc
        T = norm_tmps()
        H2 = A([2, 8, 512], BF16)
        HH = A([22, 1024], BF16)
        W1 = [A([8, 2, 256], BF16), A([8, 2, 256], BF16)]
        W2 = [A([22, 256], BF16), A([22, 256], BF16)]
        SG = [A([512], BF16), A([512], BF16)]
        w1v = d["w_ffn_in"][l].rearrange("(k p) n -> p k n", p=128)
        w2v = d["w_ffn_out"][l].rearrange("(k p) n -> p k n", p=128)
        cnt = 0
        for hf in range(2):
            for sub in range(2):
                norm_block(l, 1, 2 * hf + sub, H2[:, sub], ("H2", sub), T)
            for fg in range(11):
                ntile = 2
                w = W1[fg % 2]
                wn = ("W1", fg % 2)
                wd = ntile * 128
                kb.dma("pool", w[:, :, 0, 0:wd], w1v[:, :, fg * 256:fg * 256 + wd], [], [wn], wn)
                kb.dma("pool", w[:, :, 1, 0:wd], w1v[:, :, D_FF + fg * 256:D_FF + fg * 256 + wd], [], [wn], wn)
                for ft in range(ntile):
                    f = fg * 2 + ft
                    for sub in range(2):
                        gb, ub = (0, 1) if cnt % 2 == 0 else (2, 3)
                        for k in range(8):
                            kb.mm(ps[:, gb, :], w[:, k, 0, ft * 128:(ft + 1) * 128], H2[:, sub, k, :], k == 0, k == 7,
                                  [wn, (("H2", sub), k)], [("ps", gb)])
                        for k in range(8):
                            kb.mm(ps[:, ub, :], w[:, k, 1, ft * 128:(ft + 1) * 128], H2[:, sub, k, :], k == 0, k == 7,
                                  [wn, (("H2", sub), k)], [("ps", ub)])
                        sg = SG[cnt % 2]
                        kb.act(sg, ps[:, gb, :], AF.Silu, [("ps", gb)], [("SG", cnt % 2)])
                        kb.tt("dve", HH[:, f, sub * 512:(sub + 1) * 512], sg, ps[:, ub, :], MUL,
                              [("SG", cnt % 2), ("ps", ub)], [("HH", f, sub)])
                        cnt += 1
            for mp in range(4):
                w = W2[mp % 2]
                wn = ("W2", mp % 2)
                kb.dma("pool", w, w2v[:, :, mp * 256:(mp + 1) * 256], [], [wn], wn)
                for mi in range(2):
                    m = 2 * mp + mi
                    for sub in range(2):
                        blk = 2 * hf + sub
                        cs = slice(blk * 512, blk * 512 + 512)
                        bank = 4 + (2 * mi + sub) % 4
                        for f in range(22):
                            kb.mm(ps[:, bank, :], w[:, f, mi * 128:(mi + 1) * 128], HH[:, f, sub * 512:(sub + 1) * 512],
                                  f == 0, f == 21, [wn, ("HH", f, sub)], [("ps", bank)])
                        kb.stt(X[:, m, cs], ps[:, bank, :], DER[:, l, 5, m:m + 1], X[:, m, cs], MUL, ADD,
                               [("ps", bank), ("DER", l), Xn(m, blk)], [Xn(m, blk)])
        P.barrier()
        AR.reset(mk)

    for l in range(DEPTH):
        phase_ssm(l)
        mk = AR.mark()
        ATT = AR.alloc([8, NT], BF16, parts=64)
        if l == 0:
            WM1 = [AR.alloc([8, 256], BF16), AR.alloc([8, 256], BF16)]
            pieces1, fin1 = mods_pieces(1, WM1, 256)

            def hook(u):
                if u == 0:
                    pieces1[0][0]()
                    pieces1[1][0]()
                if u < 24:
                    pieces1[u][1]()
                    if u + 2 < 24:
                        pieces1[u + 2][0]()
                if u == 24:
                    fin1()
        else:
            def hook(u):
                pass
        phase_att(l, ATT, hook)
        if l == 0:
            dbg("att", ATT, [8, NT], BF16, [("ATT", h, b) for h in range(8) for b in range(4)], parts=64)
        phase_out(l, ATT)
        if l == 0:
            dbg("x_mid", X, [8, NT], F32, [Xn(k, b) for k in range(8) for b in range(4)])
        AR.reset(mk)
        phase_ffn(l)
        if l == 0:
            dbg("x1", X, [8, NT], F32, [Xn(k, b) for k in range(8) for b in range(4)])

    T = norm_tmps()
    YO = [AR.alloc([512], F32), AR.alloc([512], F32)]
    yv = d["yT"].rearrange("(k p) n -> p k n", p=128)
    for blk in range(4):
        cs = slice(blk * 512, blk * 512 + 512)
        for k in range(8):
            sq = T["sq"][k % 2]
            kb.act(sq, X[:, k, cs], AF.Square, [Xn(k, blk)], [("sq", k % 2)])
            kb.mm(ps[:, 0, :], ones_b, sq, k == 0, k == 7, ["ones_b", ("sq", k % 2)], [("ps", 0)])
        kb.act(T["lnv"], ps[:, 0, :], AF.Ln, [("ps", 0), "epsT"], ["lnv"], bias=epsT, scale=1.0 / D_MODEL)
        kb.act(T["rstd"], T["lnv"], AF.Exp, ["lnv"], ["rstd"], scale=-0.5)
        for k in range(8):
            yo = YO[k % 2]
            kb.stt(yo, X[:, k, cs], fnorm[:, k:k + 1], T["rstd"], MUL, MUL, [Xn(k, blk), "fnorm", "rstd"], [("YO", k % 2)])
            kb.dma("sp", yv[:, k, cs], yo, [("YO", k % 2)], [("yT", k, blk)], "out")
    return kb, xoff


def _perm():
    j = np.arange(NT)
    return 8 * (j % NCH) + j // NCH


def _rope_tables():
    n = np.arange(NT, dtype=np.float32)
    row = np.floor(n / 64.0).astype(np.float32)
    col = (n - 64.0 * row).astype(np.float32)
    inv = (10000.0 ** (-np.arange(0, 32, 2, dtype=np.float32) / 32.0)).astype(np.float32)
    ang = np.concatenate([row[:, None] * inv, col[:, None] * inv], axis=-1)
    cos, sin = np.cos(ang).astype(np.float32), np.sin(ang).astype(np.float32)
    C = np.repeat(cos, 2, axis=1)
    S = np.repeat(sin, 2, axis=1)
    S[:, 0::2] *= -1.0
    return C, S


def prep_inputs(inp):
    f = lambda a: np.ascontiguousarray(a, dtype=np.float32)
    perm = _perm()
    L = DEPTH

    def gpl(a):
        return a

    def a_lay(a):
        x = a.reshape(L, 2, 16, 2, 64)
        return f(x.transpose(0, 1, 3, 4, 2).reshape(L, 2, 128, 16))

    def b_lay(b):
        x = b.reshape(L, 2, 16, 2, 64, 16)
        return f(x.transpose(0, 1, 3, 4, 2, 5).reshape(L, 2, 128, 16, 16))

    def c_lay(c):
        x = c.reshape(L, 2, 16, 2, 16, 64)
        return f(x.transpose(0, 1, 3, 5, 2, 4).reshape(L, 2, 128, 16, 16))

    ldt = np.broadcast_to(inp["ssm_log_dt"][:, :, :, None], (L, 2, 32, 64))
    shared = {
        "w_mod": f(inp["w_mod"]),
        "b_modT": f(inp["b_mod"].reshape(L, 48, 128).transpose(0, 2, 1)),
        "norm1T": f(inp["norm1"].reshape(L, 8, 128).transpose(0, 2, 1)),
        "norm2T": f(inp["norm2"].reshape(L, 8, 128).transpose(0, 2, 1)),
        "fnormT": f(inp["final_norm"].reshape(8, 128).T),
        "w_in": f(inp["w_in"]),
        "qkg": f(np.stack([inp["q_norm"], inp["k_norm"]], axis=-1)),
        "a_reT": a_lay(inp["ssm_a_re"]), "a_imT": a_lay(inp["ssm_a_im"]), "ldtT": a_lay(ldt),
        "b_reT": b_lay(inp["ssm_b_re"]), "b_imT": b_lay(inp["ssm_b_im"]),
        "c_reT": c_lay(inp["ssm_c_re"]), "c_imT": c_lay(inp["ssm_c_im"]),
        "dS": f(np.broadcast_to(inp["ssm_d"].reshape(L, 1, 32, 16).transpose(0, 1, 3, 2), (L, 8, 16, 32)).reshape(L, 128, 32)),
        "w_glu": f(inp["w_glu"]),
        "b_gluT": f(inp["b_glu"].reshape(L, 4, 128).transpose(0, 2, 1)),
        "w_out": f(inp["w_out"]), "w_ffn_in": f(inp["w_ffn_in"]), "w_ffn_out": f(inp["w_ffn_out"]),
        "ident": np.eye(128, dtype=np.float32),
    }
    s_idx = np.arange(128) // 16
    shared["maskL"] = (s_idx[:, None] <= s_idx[None, :]).astype(np.float32)
    shared["maskU"] = (s_idx[:, None] >= s_idx[None, :]).astype(np.float32)
    psw = np.zeros((64, 64), np.float32)
    psw[np.arange(64), np.arange(64) ^ 1] = 1.0
    shared["psw"] = psw
    C, S = _rope_tables()
    in_maps = []
    for ci in range(8):
        m = dict(shared)
        sample = ci < 4
        if sample:
            xc = inp["x_sample"][ci]
            cvec = inp["c"][ci]
        else:
            xc = inp["x_prompt"][8 * (ci - 4):8 * (ci - 4) + 8].reshape(NT, D_MODEL)
            cvec = inp["c_ctx"]
        m["xT"] = f(xc[perm].T)
        m["cond"] = f(cvec.reshape(8, 128).T)
        m["flag"] = np.full((128, 1), 1.0 if sample else 0.0, np.float32)
        qaug = np.zeros((16, NT), np.float32)
        kaug = np.zeros((16, NKEY), np.float32)
        if sample:
            st = inp["state_ssm"][ci]
            x = st.reshape(L, 2, 2, 16, 2, 64)
            m["h0T"] = f(x.transpose(0, 4, 5, 3, 1, 2).reshape(L, 128, 64))
            m["ropeC"], m["ropeS"] = f(C[perm].T), f(S[perm].T)
            kaug[0, :NT] = 1.0
            kaug[8, NT:] = 1.0
            m["kcT"] = f(inp["cache_k"][ci].reshape(L, 512, 128).transpose(0, 2, 1))
            m["vc"] = f(inp["cache_v"][ci].reshape(L, 512, 128))
        else:
            m["h0T"] = np.zeros((L, 128, 64), np.float32)
            m["ropeC"], m["ropeS"] = np.ones((64, NT), np.float32), np.zeros((64, NT), np.float32)
            seg = perm // 256
            kaug[seg, np.arange(NT)] = 1.0
            kaug[8, NT:] = 1.0
            qaug[:8, :] = -BIG
            qaug[seg, np.arange(NT)] = 0.0
            qaug[8, :] = -BIG
            m["kcT"] = np.zeros((L, 128, 512), np.float32)
            m["vc"] = np.zeros((L, 512, 128), np.float32)
        m["qaug"], m["kaug"] = qaug, kaug
        in_maps.append(m)
    return in_maps


_CACHE = {}


def kernel(**inputs):
    inp = {k: np.asarray(v) for k, v in inputs.items()}
    in_maps = prep_inputs(inp)
    kb, _ = build()
    kb.P.emit()
    res = run_bass_kernel_spmd(kb.nc, in_maps, core_ids=list(range(8)))
    R = res.results
    perm = _perm()
    inv = np.empty(NT, np.int64)
    inv[perm] = np.arange(NT)
    y_sample = np.zeros((4, NT, D_MODEL), np.float32)
    y_prompt = np.zeros((32, 256, D_MODEL), np.float32)
    new_k = np.zeros((32, DEPTH, 256, 2, 64), np.float32)
    new_v = np.zeros((32, DEPTH, 256, 2, 64), np.float32)
    new_s = np.zeros((32, DEPTH, 2, 2, 32, 64), np.float32)
    for ci in range(8):
        r = R[ci]
        y = np.asarray(r["yT"]).T[inv]
        if ci < 4:
            y_sample[ci] = y
        else:
            b0 = 8 * (ci - 4)
            y_prompt[b0:b0 + 8] = y.reshape(8, 256, D_MODEL)
            kT = np.asarray(r["kT_out"])
            vv = np.asarray(r["v_out"])
            st = np.asarray(r["st_out"]).reshape(DEPTH, 2, 64, 16, 2, 2, 8)
            for l in range(DEPTH):
                new_k[b0:b0 + 8, l] = kT[l].T[inv].reshape(8, 256, 2, 64)
                new_v[b0:b0 + 8, l] = vv[l][inv].reshape(8, 256, 2, 64)
            new_s[b0:b0 + 8] = st.transpose(6, 0, 4, 5, 3, 1, 2).reshape(8, DEPTH, 2, 2, 32, 64)
    return (y_prompt, y_sample, new_k, new_v, new_s)
```

```python
import numpy as np
import contextlib
import concourse.bass as bass
import concourse.mybir as mybir
from concourse.bass_utils import run_bass_kernel_spmd

F32 = mybir.dt.float32
BF16 = mybir.dt.bfloat16
U8 = mybir.dt.uint8
ALU = mybir.AluOpType
AF = mybir.ActivationFunctionType

ENGS = ("pe", "act", "dve", "pool", "sp")
SYNC_SAME = {"pool": True, "act": True, "dve": True, "sp": True, "pe": False}

D_MODEL = 1024
DEPTH = 2
NT = 2048
NCH = 256
D_FF = 2816
NKEY = 2560
EPS = 1e-6
BIG = 32768.0


class Op:
    __slots__ = ("eng", "fn", "reads", "writes", "dma", "key", "idx", "waits", "signal", "cnt", "bar")

    def __init__(self, eng, fn, reads, writes, dma, key):
        self.eng, self.fn, self.reads, self.writes, self.dma, self.key = eng, fn, reads, writes, dma, key
        self.waits = {}
        self.signal = False
        self.cnt = 0
        self.bar = False


class Prog:
    def __init__(self, nc):
        self.nc = nc
        self.ops = []

    def op(self, eng, fn, reads=(), writes=()):
        o = Op(eng, fn, tuple(reads), tuple(writes), False, None)
        o.idx = len(self.ops)
        self.ops.append(o)
        return o

    def dma(self, eng, fn, reads=(), writes=(), key=None):
        assert key is not None
        o = Op(eng, fn, tuple(reads), tuple(writes), True, key)
        o.idx = len(self.ops)
        self.ops.append(o)
        return o

    def barrier(self):
        o = Op(None, None, (), (), False, None)
        o.bar = True
        o.idx = len(self.ops)
        self.ops.append(o)

    @staticmethod
    def _skip_same(do, o):
        return do.eng == o.eng and (not SYNC_SAME[do.eng] or (do.eng == "pe" and not o.dma))

    def resolve(self):
        last_w, readers = {}, {}
        deps_of = []
        for o in self.ops:
            deps = set()
            if not o.bar:
                for b in o.reads:
                    if b in last_w:
                        deps.add(last_w[b])
                for b in o.writes:
                    if b in last_w:
                        deps.add(last_w[b])
                    for r in readers.get(b, ()):
                        deps.add(r)
                deps.discard(o.idx)
                for b in o.reads:
                    readers.setdefault(b, []).append(o.idx)
                for b in o.writes:
                    last_w[b] = o.idx
                    readers[b] = []
            deps_of.append(deps)
        last_nd = {e: None for e in ENGS}
        for o in self.ops:
            if o.bar:
                for e in ENGS:
                    if last_nd[e] is not None:
                        self.ops[last_nd[e]].signal = True
                continue
            if not o.dma:
                last_nd[o.eng] = o.idx
            for d in deps_of[o.idx]:
                do = self.ops[d]
                if do.dma or self._skip_same(do, o):
                    continue
                do.signal = True
        eng_cnt = {e: 0 for e in ENGS}
        key_running = {}
        pending = {e: {} for e in ENGS}
        for o in self.ops:
            if o.bar:
                bw = {("eng", e): eng_cnt[e] for e in ENGS if eng_cnt[e] > 0}
                for k, v in key_running.items():
                    bw[("dma", k)] = v
                for e in ENGS:
                    for k, v in bw.items():
                        if k == ("eng", e):
                            continue
                        pending[e][k] = max(pending[e].get(k, 0), v)
                continue
            if o.dma:
                key_running[o.key] = key_running.get(o.key, 0) + 16
                o.cnt = key_running[o.key]
            elif o.signal:
                eng_cnt[o.eng] += 1
                o.cnt = eng_cnt[o.eng]
            waits = dict(pending[o.eng])
            pending[o.eng] = {}
            for d in deps_of[o.idx]:
                do = self.ops[d]
                if do.dma:
                    k = ("dma", do.key)
                    v = do.cnt if (o.dma and o.key == do.key) else key_running[do.key]
                    waits[k] = max(waits.get(k, 0), v)
                else:
                    if self._skip_same(do, o):
                        continue
                    k = ("eng", do.eng)
                    waits[k] = max(waits.get(k, 0), do.cnt)
            o.waits = waits
        self.keys = sorted(key_running.keys(), key=str)
        self.key_total = key_running

    def emit(self):
        nc = self.nc
        self.resolve()
        with contextlib.ExitStack() as st:
            sems = {}
            for e in ENGS:
                sems[("eng", e)] = st.enter_context(nc.semaphore("s_" + e))
            for i, k in enumerate(self.keys):
                sems[("dma", k)] = st.enter_context(nc.semaphore("d%d" % i))
            block = st.enter_context(nc.Block())
            engmap = {"pe": block.tensor, "act": block.scalar, "dve": block.vector,
                      "pool": block.gpsimd, "sp": block.sync}
            ops = self.ops
            key_total = self.key_total
            keys = self.keys

            def make(ename):
                def body(eng):
                    waited = {}
                    for o in ops:
                        if o.bar or o.eng != ename:
                            continue
                        for k, v in o.waits.items():
                            if waited.get(k, 0) >= v:
                                continue
                            eng.wait_ge(sems[k], v)
                            waited[k] = v
                        ins = o.fn(eng)
                        if o.dma:
                            ins.then_inc(sems[("dma", o.key)], 16)
                        elif o.signal:
                            ins.then_inc(sems[("eng", o.eng)], 1)
                    if ename == "sp":
                        for k in keys:
                            eng.wait_ge(sems[("dma", k)], key_total[k])
                return body

            for e in ENGS:
                engmap[e](make(e))


class Arena:
    def __init__(self, nc, size):
        self.ap = nc.alloc_sbuf_tensor("arena", [128, size], U8).ap()
        self.size = size
        self.off = 0

    def alloc(self, shape, dt, parts=128):
        esz = 4 if dt == F32 else 2
        n = int(np.prod(shape)) * esz
        off = (self.off + 63) // 64 * 64
        assert off + n <= self.size, ("arena overflow", off, n, self.size)
        self.off = off + n
        v = self.ap[0:parts, off:off + n].bitcast(dt)
        if len(shape) == 2:
            v = v.rearrange("p (a b) -> p a b", a=shape[0])
        elif len(shape) == 3:
            v = v.rearrange("p (a b c) -> p a b c", a=shape[0], b=shape[1])
        elif len(shape) == 4:
            v = v.rearrange("p (a b c d) -> p a b c d", a=shape[0], b=shape[1], c=shape[2])
        elif len(shape) == 5:
            v = v.rearrange("p (a b c d e) -> p a b c d e", a=shape[0], b=shape[1], c=shape[2], d=shape[3])
        return v

    def mark(self):
        return self.off

    def reset(self, m):
        self.off = m


NSM = 480 + 2048 + 288 + 64 + 64


class KB:
    def __init__(self):
        self.nc = bass.Bass("TRN2", target_bir_lowering=False)
        self.P = Prog(self.nc)
        self.AR = Arena(self.nc, 212480)
        self.ps = self.nc.alloc_psum_tensor("ps", [128, 8, 512], F32).ap()
        self.d = {}
        self.uid = 0

    def nm(self, s):
        self.uid += 1
        return "%s#%d" % (s, self.uid)

    def din(self, n, shape):
        self.d[n] = self.nc.dram_tensor(n, list(shape), F32, kind="ExternalInput").ap()
        return self.d[n]

    def dout(self, n, shape):
        self.d[n] = self.nc.dram_tensor(n, list(shape), F32, kind="ExternalOutput").ap()
        return self.d[n]

    def dscr(self, n, shape, dt, ext=False):
        if ext:
            self.d[n] = self.nc.dram_tensor(n, list(shape), dt, kind="ExternalOutput").ap()
        else:
            self.d[n] = self.nc.dram_tensor(n, list(shape), dt).ap()
        return self.d[n]

    def tt(self, eng, out, a, b, op, r, w):
        self.P.op(eng, lambda e: e.tensor_tensor(out=out, in0=a, in1=b, op=op), r, w)

    def ts(self, eng, out, a, s1, op0, r, w, s2=None, op1=None):
        if s2 is None:
            self.P.op(eng, lambda e: e.tensor_scalar(out=out, in0=a, scalar1=s1, scalar2=None, op0=op0), r, w)
        else:
            self.P.op(eng, lambda e: e.tensor_scalar(out=out, in0=a, scalar1=s1, scalar2=s2, op0=op0, op1=op1), r, w)

    def stt(self, out, a, s, b, op0, op1, r, w):
        self.P.op("dve", lambda e: e.scalar_tensor_tensor(out=out, in0=a, scalar=s, in1=b, op0=op0, op1=op1), r, w)

    def cp(self, eng, out, a, r, w):
        self.P.op(eng, lambda e: e.tensor_copy(out=out, in_=a), r, w)

    def act(self, out, a, func, r, w, bias=None, scale=None):
        kw = {}
        if bias is not None:
            kw["bias"] = bias
        if scale is not None:
            kw["scale"] = scale
        self.P.op("act", lambda e: e.activation(out=out, in_=a, func=func, **kw), r, w)

    def mm(self, out, lhsT, rhs, start, stop, r, w):
        self.P.op("pe", lambda e: e.matmul(out, lhsT=lhsT, rhs=rhs, start=start, stop=stop), r, w)

    def mm_hi(self, out, lhsT, rhs, start, stop, r, w):
        self.mm(out[0:64], lhsT[:, 0:64], rhs, start, stop, r, w)
        self.mm(out[64:128], lhsT[:, 64:128], rhs, start, stop, r, w)

    def tr(self, out, a, ident, r, w):
        self.P.op("pe", lambda e: e.transpose(out=out, in_=a, identity=ident), r, w)

    def ms(self, eng, out, val, w):
        self.P.op(eng, lambda e: e.memset(out, val), (), w)

    def rcp(self, out, a, r, w):
        self.P.op("dve", lambda e: e.reciprocal(out=out, in_=a), r, w)

    def dma(self, q, out, a, r, w, key):
        self.P.dma(q, lambda e: e.dma_start(out=out, in_=a), r, w, key)


def bc(ap, shape, axis):
    return ap.unsqueeze(axis).to_broadcast(list(shape))


DEBUG = False


def build():
    kb = KB()
    nc, P, AR, ps, d = kb.nc, kb.P, kb.AR, kb.ps, kb.d
    MUL, ADD, SUB = ALU.mult, ALU.add, ALU.subtract
    for n, s in (("xT", [1024, NT]), ("cond", [128, 8]), ("flag", [128, 1]), ("w_mod", [2, 1024, 6144]),
                 ("b_modT", [2, 128, 48]), ("norm1T", [2, 128, 8]), ("norm2T", [2, 128, 8]), ("fnormT", [128, 8]),
                 ("w_in", [2, 1024, 1280]), ("qkg", [2, 64, 2]), ("a_reT", [2, 2, 128, 16]), ("a_imT", [2, 2, 128, 16]),
                 ("ldtT", [2, 2, 128, 16]), ("b_reT", [2, 2, 128, 16, 16]), ("b_imT", [2, 2, 128, 16, 16]),
                 ("c_reT", [2, 2, 128, 16, 16]), ("c_imT", [2, 2, 128, 16, 16]), ("dS", [2, 128, 32]),
                 ("h0T", [2, 128, 64]), ("w_glu", [2, 512, 512]), ("b_gluT", [2, 128, 4]), ("w_out", [2, 1024, 1024]),
                 ("w_ffn_in", [2, 1024, 5632]), ("w_ffn_out", [2, 2816, 1024]), ("ropeC", [64, NT]), ("ropeS", [64, NT]),
                 ("qaug", [16, NT]), ("kaug", [16, NKEY]), ("kcT", [2, 128, 512]), ("vc", [2, 512, 128]),
                 ("ident", [128, 128]), ("maskL", [128, 128]), ("maskU", [128, 128]), ("psw", [64, 64])):
        kb.din(n, s)
    kb.dout("yT", [1024, NT])
    kb.dout("kT_out", [2, 128, NT])
    kb.dout("v_out", [2, NT, 128])
    kb.dout("st_out", [2, 128, 512])
    kb.dscr("MGd", [2, 128, 4096], BF16, ext=True)
    kb.dscr("WBd", [2, 2, 128, 4096], BF16, ext=True)
    kb.dscr("QDd", [2, 2, 128, 4096], BF16, ext=True)
    kb.dscr("SMd", [2, 128, NSM], F32, ext=True)
    kb.dscr("u_scr", [512, NT], BF16)
    kb.dscr("y_scr", [32, 128, 256], BF16)
    kb.dscr("hb_scr", [4, 128, 4096], BF16)
    dbgc = [0]

    def dbg(name, ap, shape, dt, reads, parts=128):
        if not DEBUG:
            return
        t = kb.nc.dram_tensor("dbg_" + name, [parts] + list(shape), dt, kind="ExternalOutput").ap()
        kb.dma("sp", t, ap, reads, [("dbg", name)], "dbg")

    ident = AR.alloc([128], F32)
    ones_b = AR.alloc([128], BF16)
    psw_b = AR.alloc([64], BF16, parts=64)
    flag = AR.alloc([1], F32)
    halfpi = AR.alloc([1], F32)
    epsT = AR.alloc([1], F32)
    cond = AR.alloc([8], F32)
    condS = AR.alloc([8], BF16)
    MODS = AR.alloc([2, 48], F32)
    DER = AR.alloc([2, 6, 8], F32)
    n1 = AR.alloc([2, 8], F32)
    n2 = AR.alloc([2, 8], F32)
    fnorm = AR.alloc([8], F32)
    qkg = AR.alloc([2, 2], F32, parts=64)
    bglu = AR.alloc([2, 4], F32)
    bmod = AR.alloc([2, 48], F32)
    dSt = AR.alloc([2, 32], F32)
    kb.dma("sp", ident, d["ident"], [], ["ident"], "c0")
    kb.dma("sp", flag, d["flag"], [], ["flag"], "c0")
    kb.dma("sp", cond, d["cond"], [], ["cond"], "c0")
    kb.dma("sp", n1, d["norm1T"].rearrange("l p k -> p l k"), [], ["n1"], "c0")
    kb.dma("sp", n2, d["norm2T"].rearrange("l p k -> p l k"), [], ["n2"], "c0")
    kb.dma("sp", fnorm, d["fnormT"], [], ["fnorm"], "c0")
    kb.dma("sp", qkg, d["qkg"].rearrange("l p k -> p l k"), [], ["qkg"], "c0")
    kb.dma("sp", bglu, d["b_gluT"].rearrange("l p k -> p l k"), [], ["bglu"], "c0")
    kb.dma("sp", bmod, d["b_modT"].rearrange("l p k -> p l k"), [], ["bmod"], "c0")
    kb.dma("sp", dSt, d["dS"].rearrange("l p k -> p l k"), [], ["dSt"], "c0")
    kb.dma("pool", psw_b, d["psw"], [], ["psw_b"], "c1")
    kb.ms("dve", ones_b, 1.0, ["ones_b"])
    kb.ms("dve", halfpi, float(np.pi / 2), ["halfpi"])
    kb.ms("dve", epsT, EPS, ["epsT"])
    kb.act(condS, cond, AF.Silu, ["cond"], ["condS"])
    xoff = AR.mark()

    def mods_pieces(l, WM, pw=512):
        wv = d["w_mod"][l].rearrange("(k p) n -> p k n", p=128)
        pieces = []
        mt = pw // 128
        for i in range(6144 // pw):
            def dma_fn(i=i):
                kb.dma("pool", WM[i % 2], wv[:, :, i * pw:(i + 1) * pw], [], [("WM", i % 2)], ("WM", i % 2))

            def mm_fn(i=i):
                for m in range(mt):
                    col = i * mt + m
                    for k in range(8):
                        kb.mm(ps[:, 0, col:col + 1], WM[i % 2][:, k, m * 128:(m + 1) * 128], condS[:, k:k + 1],
                              k == 0, k == 7, [("WM", i % 2), "condS"], [("ps", 0)])
                kb.cp("dve", MODS[:, l, i * mt:(i + 1) * mt], ps[:, 0, i * mt:(i + 1) * mt], [("ps", 0)], [("MODS", l)])
            pieces.append((dma_fn, mm_fn))

        def fin():
            kb.tt("dve", MODS[:, l, :], MODS[:, l, :], bmod[:, l, :], ADD, [("MODS", l), "bmod"], [("MODS", l)])
            nn = [n1, n2]
            for w in range(2):
                kb.stt(DER[:, l, 3 * w + 0, :], MODS[:, l, 24 * w + 8:24 * w + 16], 1.0, nn[w][:, l, :], ADD, MUL,
                       [("MODS", l), "n1", "n2"], [("DER", l)])
                kb.cp("dve", DER[:, l, 3 * w + 1, :], MODS[:, l, 24 * w:24 * w + 8], [("MODS", l)], [("DER", l)])
                kb.cp("dve", DER[:, l, 3 * w + 2, :], MODS[:, l, 24 * w + 16:24 * w + 24], [("MODS", l)], [("DER", l)])
        return pieces, fin

    def ssm_pre(l):
        mk = AR.mark()
        A = AR.alloc
        maskL = A([128], F32)
        maskU = A([128], F32)
        kb.dma("sp", maskL, d["maskL"], [], ["maskL"], "pre")
        kb.dma("sp", maskU, d["maskU"], [], ["maskU"], "pre")
        are, aim, ldt = A([2, 16], F32), A([2, 16], F32), A([2, 16], F32)
        kb.dma("sp", are, d["a_reT"][l].rearrange("d p g -> p d g"), [], ["are"], "pre")
        kb.dma("sp", aim, d["a_imT"][l].rearrange("d p g -> p d g"), [], ["aim"], "pre")
        kb.dma("sp", ldt, d["ldtT"][l].rearrange("d p g -> p d g"), [], ["ldt"], "pre")
        br, bi, cr, ci = (A([2, 16, 16], F32) for _ in range(4))
        for t, n in ((br, "b_reT"), (bi, "b_imT"), (cr, "c_reT"), (ci, "c_imT")):
            kb.dma("sp", t, d[n][l].rearrange("d p g h -> p d g h"), [], [n], "pre")
        SM = A([NSM], F32)
        LV = SM[:, 0:480].rearrange("p (g d k j) -> p g d k j", g=16, d=2, k=5)
        FX = SM[:, 480:2528].rearrange("p (g d r i) -> p g d r i", g=16, d=2, r=2)
        CBm = SM[:, 2528:2816].rearrange("p (g d k j) -> p g d k j", g=16, d=2, k=3)
        EBH0 = SM[:, 2816:2880].rearrange("p (g d r) -> p g d r", g=16, d=2)
        H0 = SM[:, 2880:2944].rearrange("p (g d r) -> p g d r", g=16, d=2)
        kb.dma("sp", SM[:, 2880:2944], d["h0T"][l], [], ["H0"], "pre")

        cnt = [0]

        def small(name):
            cnt[0] += 1
            return A([2, 16], F32), "%s_%d_%d" % (name, l, cnt[0])

        def V(eng, out, a, b, op):
            kb.tt(eng, out[0], a[0], b[0], op, [a[1], b[1]], [out[1]])

        dt_ = small("dt")
        kb.act(dt_[0], ldt, AF.Exp, ["ldt"], [dt_[1]])
        e_ = small("e")
        ang = small("ang")
        V("dve", e_, (are, "are"), dt_, MUL)
        V("dve", ang, (aim, "aim"), dt_, MUL)
        mag, magn, c16, s16 = small("mag"), small("magn"), small("c16"), small("s16")
        kb.act(mag[0], e_[0], AF.Exp, [e_[1]], [mag[1]], scale=1.0 / 16)
        kb.act(magn[0], e_[0], AF.Exp, [e_[1]], [magn[1]], scale=-1.0 / 16)
        kb.act(c16[0], ang[0], AF.Sin, [ang[1], "halfpi"], [c16[1]], bias=halfpi, scale=1.0 / 16)
        kb.act(s16[0], ang[0], AF.Sin, [ang[1]], [s16[1]], scale=1.0 / 16)
        E = {}

        def cmul(a, b):
            (ar_, ai_), (br_, bi_) = a, b
            t1, t2, rr, ii = small("t1"), small("t2"), small("rr"), small("ii")
            t3, t4 = small("t3"), small("t4")
            V("dve", t1, ar_, br_, MUL)
            V("dve", t2, ai_, bi_, MUL)
            V("dve", t3, ar_, bi_, MUL)
            V("dve", t4, ai_, br_, MUL)
            V("dve", rr, t1, t2, SUB)
            V("dve", ii, t3, t4, ADD)
            return (rr, ii)

        mu_r, mu_i, nu_r, nu_i = small("mur"), small("mui"), small("nur"), small("nui")
        V("dve", mu_r, mag, c16, MUL)
        V("dve", mu_i, mag, s16, MUL)
        V("dve", nu_r, magn, c16, MUL)
        kb.stt(nu_i[0], magn[0], -1.0, s16[0], MUL, MUL, [magn[1], s16[1]], [nu_i[1]])
        cur, curn = (mu_r, mu_i), (nu_r, nu_i)
        for _ in range(4):
            cur = cmul(cur, cur)
            curn = cmul(curn, curn)
        E[1], E[-1] = cur, curn
        for j in range(2, 9):
            E[j] = cmul(E[j - 1], E[1])
        for j in range(2, 8):
            E[-j] = cmul(E[-j + 1], E[-1])
        E[0] = (small("one"), small("zero"))
        kb.ms("dve", E[0][0][0], 1.0, [E[0][0][1]])
        kb.ms("dve", E[0][1][0], 0.0, [E[0][1][1]])
        j = 8
        while j < 1024:
            E[2 * j] = cmul(E[j], E[j])
            j *= 2
        t1, t2, den, rden, lm1 = small("za"), small("zb"), small("den"), small("rden"), small("lm1")
        V("dve", t1, (are, "are"), (are, "are"), MUL)
        V("dve", t2, (aim, "aim"), (aim, "aim"), MUL)
        V("dve", den, t1, t2, ADD)
        kb.rcp(rden[0], den[0], [den[1]], [rden[1]])
        kb.ts("dve", lm1[0], E[1][0][0], -1.0, ADD, [E[1][0][1]], [lm1[1]])
        t3, t4, t5, t6, zr, zi = small("zc"), small("zd"), small("ze"), small("zf"), small("zr"), small("zi")
        V("dve", t3, lm1, (are, "are"), MUL)
        V("dve", t4, E[1][1], (aim, "aim"), MUL)
        V("dve", t5, t3, t4, ADD)
        V("dve", zr, t5, rden, MUL)
        V("dve", t3, E[1][1], (are, "are"), MUL)
        V("dve", t4, lm1, (aim, "aim"), MUL)
        V("dve", t6, t3, t4, SUB)
        V("dve", zi, t6, rden, MUL)
        bbr, bbi, w1, w2 = (A([2, 16, 16], F32) for _ in range(4))
        sh = [128, 2, 16, 16]
        zrb, zib = bc(zr[0], sh, 3), bc(zi[0], sh, 3)
        kb.tt("dve", w1, br, zrb, MUL, ["b_reT", zr[1]], ["w1"])
        kb.tt("dve", w2, bi, zib, MUL, ["b_imT", zi[1]], ["w2"])
        kb.tt("dve", bbr, w1, w2, SUB, ["w1", "w2"], ["bbr"])
        kb.tt("dve", w1, bi, zrb, MUL, ["b_imT", zr[1]], ["w1"])
        kb.tt("dve", w2, br, zib, MUL, ["b_reT", zi[1]], ["w2"])
        kb.tt("dve", bbi, w1, w2, ADD, ["w1", "w2"], ["bbi"])

        slots = [(A([16, 8, 16], F32), A([16, 8, 16], F32)) for _ in range(4)]
        tmp = {"dve": [A([16, 16], F32) for _ in range(4)], "pool": [A([16, 16], F32) for _ in range(4)]}

        def table(eng, slot, dirn, Vr, Vi, vnames, expo, neg):
            Tr, Ti = slots[slot]
            tn = ("slot", slot)
            s3 = [128, 16, 16]
            for pos in range(8):
                q1, q2, q3, q4 = tmp[eng]
                qn1, qn2, qn3, qn4 = ("tq%d%s" % (i_, eng) for i_ in range(4))
                tnp = ("slot", slot, pos)
                er, ei = E[expo[pos]]
                erb, eib = bc(er[0][:, dirn, :], s3, 2), bc(ei[0][:, dirn, :], s3, 2)
                vr, vi = Vr[:, dirn], Vi[:, dirn]
                kb.tt(eng, q1, vr, erb, MUL, [vnames[0], er[1]], [qn1])
                kb.tt(eng, q2, vi, eib, MUL, [vnames[1], ei[1]], [qn2])
                kb.tt(eng, q3, vi, erb, MUL, [vnames[1], er[1]], [qn3])
                kb.tt(eng, q4, vr, eib, MUL, [vnames[0], ei[1]], [qn4])
                kb.tt(eng, Tr[:, :, pos, :], q1, q2, SUB, [qn1, qn2], [tnp])
                if not neg:
                    kb.tt(eng, Ti[:, :, pos, :], q3, q4, ADD, [qn3, qn4], [tnp])
                else:
                    kb.tt(eng, q3, q3, q4, ADD, [qn3, qn4], [qn3])
                    kb.ts(eng, Ti[:, :, pos, :], q3, -1.0, MUL, [qn3], [tnp])

        def flat(t, rows, gp):
            return t[rows, gp].rearrange("p s h -> p (s h)")

        MGacc = A([32, 128], F32)
        MG = A([32, 128], BF16)
        Wst = A([16, 2, 128], BF16)
        Qst = A([16, 2, 128], BF16)
        tmpT = A([4, 128], F32)
        BB, CC = ("bbr", "bbi"), ("c_reT", "c_imT")
        rng8 = list(range(8))
        table("dve", 0, 0, bbr, bbi, BB, [-s for s in rng8], False)
        table("dve", 1, 0, cr, ci, CC, rng8, True)
        table("pool", 2, 1, bbr, bbi, BB, rng8, False)
        table("pool", 3, 1, cr, ci, CC, [-t for t in rng8], True)

        tb16 = [A([16, 8, 16], BF16) for _ in range(4)]

        def toeplitz(sa, sb, first):
            for i_, (sl, ri_) in enumerate(((sa, 0), (sa, 1), (sb, 0), (sb, 1))):
                kb.act(tb16[i_], slots[sl][ri_], AF.Copy, [("slot", sl, p_) for p_ in range(8)], [("tb16", i_)])
            for gq in range(8):
                bank = gq % 2
                for q in range(4):
                    g = 4 * gq + q
                    gp, g2 = g // 2, g % 2
                    rows = slice(64 * g2, 64 * g2 + 64)
                    o = ps[:, bank, q * 128:(q + 1) * 128]
                    mmf = kb.mm if g2 == 0 else kb.mm_hi
                    mmf(o, flat(tb16[0], rows, gp), flat(tb16[2], rows, gp), True, False,
                        [("tb16", 0), ("tb16", 2)], [("ps", bank)])
                    mmf(o, flat(tb16[1], rows, gp), flat(tb16[3], rows, gp), False, True,
                        [("tb16", 1), ("tb16", 3)], [("ps", bank)])
                pv = ps[:, bank, :].rearrange("p (q n) -> p q n", q=4)
                acc = MGacc[:, 4 * gq:4 * gq + 4, :]
                if first:
                    kb.tt("dve", acc, pv, bc(maskL, [128, 4, 128], 1), MUL, [("ps", bank), "maskL"], [("MGacc", gq)])
                    for q in range(4):
                        g = 4 * gq + q
                        kb.stt(MGacc[:, g, :], ident, dSt[:, l, g:g + 1], MGacc[:, g, :], MUL, ADD,
                               ["ident", "dSt", ("MGacc", gq)], [("MGacc", gq)])
                else:
                    kb.tt("dve", tmpT, pv, bc(maskU, [128, 4, 128], 1), MUL, [("ps", bank), "maskU"], ["tmpT"])
                    kb.tt("dve", MG[:, 4 * gq:4 * gq + 4, :], tmpT, acc, ADD, ["tmpT", ("MGacc", gq)], ["MG"])

        def wb_out(slot, dirn):
            for gq in range(8):
                bank = 2 + gq % 2
                for q in range(4):
                    gp, ri = 2 * gq + q // 2, q % 2
                    kb.tr(ps[:, bank, q * 128:(q + 1) * 128], flat(slots[slot][ri], slice(0, 128), gp), ident,
                          [("slot", slot, p_) for p_ in range(8)] + ["ident"], [("ps", bank)])
                kb.act(Wst[:, 2 * gq:2 * gq + 2, :, :], ps[:, bank, :].rearrange("p (g r n) -> p g r n", g=2, r=2),
                       AF.Copy, [("ps", bank)], ["Wst"])
            kb.dma("sp", d["WBd"][l, dirn], Wst.rearrange("p g r n -> p (g r n)"), ["Wst"], [("WBd", l, dirn)], "pre")

        def qd_out(slot, dirn):
            for ri in range(2):
                kb.cp("dve", Qst[:, :, ri, :], slots[slot][ri].rearrange("p g s h -> p g (s h)"),
                      [("slot", slot, p_) for p_ in range(8)], ["Qst"])
            kb.dma("sp", d["QDd"][l, dirn], Qst.rearrange("p g r n -> p (g r n)"), ["Qst"], [("QDd", l, dirn)], "pre")

        toeplitz(0, 1, True)
        table("dve", 0, 0, bbr, bbi, BB, [7 - s for s in rng8], False)
        wb_out(0, 0)
        table("dve", 1, 0, cr, ci, CC, [t + 1 for t in rng8], True)
        qd_out(1, 0)
        toeplitz(2, 3, False)
        wb_out(2, 1)
        table("pool", 3, 1, cr, ci, CC, [8 - t for t in rng8], True)
        qd_out(3, 1)
        kb.dma("sp", d["MGd"][l], MG.rearrange("p g n -> p (g n)"), ["MG"], [("MGd", l)], "pre")

        def perm(ap2):
            return ap2.rearrange("p d g -> p g d")
        for k in range(5):
            er, ei = E[8 * (2 ** k)]
            kb.cp("dve", LV[:, :, :, k, 0], perm(er[0]), [er[1]], ["SM"])
            kb.cp("dve", LV[:, :, :, k, 1], perm(ei[0]), [ei[1]], ["SM"])
            kb.ts("dve", LV[:, :, :, k, 2], perm(ei[0]), -1.0, MUL, [ei[1]], ["SM"])
        for k in range(3):
            er, ei = E[256 * (2 ** k)]
            kb.cp("dve", CBm[:, :, :, k, 0], perm(er[0]), [er[1]], ["SM"])
            kb.cp("dve", CBm[:, :, :, k, 1], perm(ei[0]), [ei[1]], ["SM"])
            kb.ts("dve", CBm[:, :, :, k, 2], perm(ei[0]), -1.0, MUL, [ei[1]], ["SM"])
        f1, f2 = A([16, 16], F32), A([16, 16], F32)

        def fx_mul(dirn, dst, src, n, ej):
            er, ei = E[ej]
            s3 = [128, 16, n]
            erb, eib = bc(er[0][:, dirn, :], s3, 2), bc(ei[0][:, dirn, :], s3, 2)
            sr, si = FX[:, :, dirn, 0, src[0]:src[1]], FX[:, :, dirn, 1, src[0]:src[1]]
            dr, di = FX[:, :, dirn, 0, dst[0]:dst[1]], FX[:, :, dirn, 1, dst[0]:dst[1]]
            a1, a2 = f1[:, :, 0:n], f2[:, :, 0:n]
            kb.tt("dve", a1, sr, erb, MUL, ["SM", er[1]], ["f1"])
            kb.tt("dve", a2, si, eib, MUL, ["SM", ei[1]], ["f2"])
            kb.tt("dve", dr, a1, a2, SUB, ["f1", "f2"], ["SM"])
            kb.tt("dve", a1, si, erb, MUL, ["SM", er[1]], ["f1"])
            kb.tt("dve", a2, sr, eib, MUL, ["SM", ei[1]], ["f2"])
            kb.tt("dve", di, a1, a2, ADD, ["f1", "f2"], ["SM"])
        for ri in range(2):
            kb.cp("dve", FX[:, :, 0, ri, 0], E[8][ri][0][:, 0, :], [E[8][ri][1]], ["SM"])
            kb.cp("dve", FX[:, :, 0, ri, 1], E[16][ri][0][:, 0, :], [E[16][ri][1]], ["SM"])
            kb.cp("dve", FX[:, :, 1, ri, 31], E[8][ri][0][:, 1, :], [E[8][ri][1]], ["SM"])
            kb.cp("dve", FX[:, :, 1, ri, 30], E[16][ri][0][:, 1, :], [E[16][ri][1]], ["SM"])
        n_ = 2
        while n_ < 32:
            fx_mul(0, (n_, 2 * n_), (0, n_), n_, 8 * n_)
            fx_mul(1, (32 - 2 * n_, 32 - n_), (32 - n_, 32), n_, 8 * n_)
            n_ *= 2
        er, ei = E[256]
        g1, g2, g3 = small("g1"), small("g2"), small("g3")
        pg = lambda x: x.rearrange("p d g -> p g d")
        h0r, h0i = H0[:, :, :, 0], H0[:, :, :, 1]
        g1p, g2p = pg(g1[0]), pg(g2[0])
        kb.tt("dve", g1p, h0r, pg(er[0]), MUL, ["H0", er[1]], [g1[1]])
        kb.tt("dve", g2p, h0i, pg(ei[0]), MUL, ["H0", ei[1]], [g2[1]])
        kb.tt("dve", EBH0[:, :, :, 0], g1p, g2p, SUB, [g1[1], g2[1]], ["SM"])
        kb.tt("dve", g1p, h0i, pg(er[0]), MUL, ["H0", er[1]], [g1[1]])
        kb.tt("dve", g2p, h0r, pg(ei[0]), MUL, ["H0", ei[1]], [g2[1]])
        kb.tt("dve", EBH0[:, :, :, 1], g1p, g2p, ADD, [g1[1], g2[1]], ["SM"])
        kb.dma("sp", d["SMd"][l], SM, ["SM", "H0"], [("SMd", l)], "pre")
        P.barrier()
        AR.reset(mk)

    WM0 = [AR.alloc([8, 512], BF16), AR.alloc([8, 512], BF16)]
    pieces, fin = mods_pieces(0, WM0)
    for dma_fn, mm_fn in pieces:
        dma_fn()
        mm_fn()
    fin()
    dbg("mods0", MODS[:, 0, :], [48], F32, [("MODS", 0)])
    dbg("der0", DER[:, 0], [6, 8], F32, [("DER", 0)])
    for l in range(DEPTH):
        ssm_pre(l)
    P.barrier()
    AR.reset(xoff)

    X = AR.alloc([8, NT], F32)
    GL = AR.alloc([4, NT], BF16)
    kb.dma("sp", X, d["xT"].rearrange("(k p) n -> p k n", p=128), [], [("X", k, b) for k in range(8) for b in range(4)], "X")
    base = AR.mark()
    Xn = lambda k, b: ("X", k, b)
    allX = lambda b: [Xn(k, b) for k in range(8)]

    def norm_block(l, w, blk, HB, hbname, T):
        cs = slice(blk * 512, blk * 512 + 512)
        for k in range(8):
            sq = T["sq"][k % 2]
            kb.act(sq, X[:, k, cs], AF.Square, [Xn(k, blk)], [("sq", k % 2)])
            kb.mm(ps[:, 0, :], ones_b, sq, k == 0, k == 7, ["ones_b", ("sq", k % 2)], [("ps", 0)])
        kb.act(T["lnv"], ps[:, 0, :], AF.Ln, [("ps", 0), "epsT"], ["lnv"], bias=epsT, scale=1.0 / D_MODEL)
        kb.act(T["rstd"], T["lnv"], AF.Exp, ["lnv"], ["rstd"], scale=-0.5)
        for k in range(8):
            tn = T["tn"][k % 2]
            kb.stt(tn, X[:, k, cs], DER[:, l, 3 * w, k:k + 1], T["rstd"], MUL, MUL,
                   [Xn(k, blk), ("DER", l), "rstd"], [("tn", k % 2)])
            kb.act(HB[:, k, :], tn, AF.Identity, [("tn", k % 2), ("DER", l)], [(hbname, k)],
                   bias=DER[:, l, 3 * w + 1, k:k + 1])

    def norm_tmps():
        return {"sq": [AR.alloc([512], BF16), AR.alloc([512], BF16)], "lnv": AR.alloc([512], F32),
                "rstd": AR.alloc([512], F32), "tn": [AR.alloc([512], F32), AR.alloc([512], F32)]}

    def proj(HB, hbname, W, wname, c0, M, out, bank):
        for k in range(8):
            kb.mm(out, W[:, k, c0:c0 + M], HB[:, k, :], k == 0, k == 7, [(hbname, k), wname], [("ps", bank)])

    def phase_ssm(l):
        mk = AR.mark()
        A = AR.alloc
        T = norm_tmps()
        HB = A([8, 512], BF16)
        WU = A([8, 512], BF16)
        UST = A([4, 512], BF16)
        UG = A([32, 256], BF16)
        mk2 = AR.mark()
        MG = A([32, 128], BF16)
        WB = A([2, 16, 2, 128], BF16)
        QD = A([2, 16, 2, 128], BF16)
        SM = A([NSM], F32)
        LV = SM[:, 0:480].rearrange("p (g d k j) -> p g d k j", g=16, d=2, k=5)
        FX = SM[:, 480:2528].rearrange("p (g d r i) -> p g d r i", g=16, d=2, r=2)
        CBm = SM[:, 2528:2816].rearrange("p (g d k j) -> p g d k j", g=16, d=2, k=3)
        EBH0 = SM[:, 2816:2880].rearrange("p (g d r) -> p g d r", g=16, d=2)
        H0 = SM[:, 2880:2944].rearrange("p (g d r) -> p g d r", g=16, d=2)
        SAB = [A([2, 2, 8, 64], F32), A([2, 2, 8, 64], F32)]
        CAB = [A([2, 2, 16], F32), A([2, 2, 16], F32)]
        CIN = A([2, 2, 8], F32)
        FT = [A([8, 32], F32), A([8, 32], F32)]
        HP = [A([2, 2, 8, 32], BF16), A([2, 2, 8, 32], BF16)]
        ST = A([16, 2, 2, 8], F32)
        YS = [A([256], BF16) for _ in range(4)]
        kb.dma("pool", WU, d["w_in"][l].rearrange("(k p) n -> p k n", p=128)[:, :, 768:1280], [], ["WU"], "WU")
        kb.dma("sp", MG, d["MGd"][l].rearrange("p (g n) -> p g n", g=32), [("MGd", l)], ["MG"], "pk")
        for dd in range(2):
            kb.dma("sp", WB[:, dd], d["WBd"][l, dd].rearrange("p (g r n) -> p g r n", g=16, r=2), [("WBd", l, dd)], [("WB", dd)], "pk")
            kb.dma("sp", QD[:, dd], d["QDd"][l, dd].rearrange("p (g r n) -> p g r n", g=16, r=2), [("QDd", l, dd)], [("QD", dd)], "pk")
        kb.dma("sp", SM, d["SMd"][l], [("SMd", l)], ["SM"], "pk")
        for t_, tn_ in zip(SAB, ("SA", "SB")):
            kb.ms("dve", t_, 0.0, [(tn_, a_, b_) for a_ in range(2) for b_ in range(2)])
        for t_ in CAB:
            kb.ms("dve", t_, 0.0, [("CA", a_, b_) for a_ in range(2) for b_ in range(2)] + [("CB", a_, b_) for a_ in range(2) for b_ in range(2)])
        for blk in range(4):
            norm_block(l, 0, blk, HB, "HB", T)
            kb.dma("sp", d["hb_scr"][blk], HB.rearrange("p k n -> p (k n)"), [("HB", k) for k in range(8)],
                   [("hb_scr", blk)], "hbw")
            if l == 0 and blk == 0:
                dbg("hb", HB, [8, 512], BF16, [("HB", k) for k in range(8)])
            for m in range(4):
                bank = 1 + m % 2
                proj(HB, "HB", WU, "WU", m * 128, 128, ps[:, bank, :], bank)
                kb.act(UST[:, m, :], ps[:, bank, :], AF.Copy, [("ps", bank)], [("UST", m)])
                kb.dma("sp", d["u_scr"][m * 128:(m + 1) * 128, blk * 512:(blk + 1) * 512], UST[:, m, :],
                       [("UST", m)], [("u_scr", blk)], "us")
            for tl in range(2):
                tau = 2 * blk + tl
                src = d["u_scr"][:, tau * 256:(tau + 1) * 256].rearrange("(g h) c -> h g c", h=16)
                kb.dma("sp", UG[16 * tau:16 * tau + 16, :, :], src, [("u_scr", blk)], ["UG"], "ug")
        if l == 0:
            dbg("ug", UG, [32, 256], BF16, ["UG"])
        for gp in range(16):
            for dd in range(2):
                bank = 5 + dd
                for ri in range(2):
                    for g2 in range(2):
                        kb.mm(ps[64 * g2:64 * g2 + 64, bank, ri * 256:(ri + 1) * 256],
                              WB[:, dd, gp, ri, 64 * g2:64 * g2 + 64], UG[:, 2 * gp + g2, :], True, True,
                              [("WB", dd), "UG"], [("ps", bank)])
                kb.act(SAB[0][:, dd, :, :, 16:48], ps[:, bank, :].rearrange("p (r b i) -> p r b i", r=2, b=8),
                       AF.Copy, [("ps", bank)], [("SA", dd, 0), ("SA", dd, 1)])
            for k in range(5):
                sh = 2 ** k
                src, dst = SAB[k % 2], SAB[(k + 1) % 2]
                sn, dn = ("SA", "SB")[k % 2], ("SA", "SB")[(k + 1) % 2]
                for step in range(4):
                    for dd in range(2):
                        lo, hi = (16 - sh, 48 - sh) if dd == 0 else (16 + sh, 48 + sh)
                        er, ei, nei = (LV[:, gp, dd, k, j:j + 1] for j in range(3))
                        sr, si = src[:, dd, 0], src[:, dd, 1]
                        dr, di = dst[:, dd, 0, :, 16:48], dst[:, dd, 1, :, 16:48]
                        dnr, dni = (dn, dd, 0), (dn, dd, 1)
                        srd = [(sn, dd, 0), (sn, dd, 1), "SM"]
                        if step == 0:
                            kb.stt(dr, sr[:, :, lo:hi], er, sr[:, :, 16:48], MUL, ADD, srd, [dnr])
                        elif step == 1:
                            kb.stt(di, si[:, :, lo:hi], er, si[:, :, 16:48], MUL, ADD, srd, [dni])
                        elif step == 2:
                            kb.stt(dr, si[:, :, lo:hi], nei, dr, MUL, ADD, srd + [dnr], [dnr])
                        else:
                            kb.stt(di, sr[:, :, lo:hi], ei, di, MUL, ADD, srd + [dni], [dni])
            Hl = SAB[1]
            kb.cp("dve", CAB[0][:, 0, :, 4:12], Hl[:, 0, :, :, 47], [("SB", a_, b_) for a_ in range(2) for b_ in range(2)], [("CA", a_, b_) for a_ in range(2) for b_ in range(2)])
            kb.cp("dve", CAB[0][:, 1, :, 4:12], Hl[:, 1, :, :, 16], [("SB", a_, b_) for a_ in range(2) for b_ in range(2)], [("CA", a_, b_) for a_ in range(2) for b_ in range(2)])
            kb.tt("dve", CAB[0][:, 0, :, 4], CAB[0][:, 0, :, 4], EBH0[:, gp, 0, :], ADD, [("CA", a_, b_) for a_ in range(2) for b_ in range(2)] + ["SM"], [("CA", a_, b_) for a_ in range(2) for b_ in range(2)])
            kb.tt("dve", CAB[0][:, 1, :, 11], CAB[0][:, 1, :, 11], EBH0[:, gp, 1, :], ADD, [("CA", a_, b_) for a_ in range(2) for b_ in range(2)] + ["SM"], [("CA", a_, b_) for a_ in range(2) for b_ in range(2)])
            for k in range(3):
                sh = 2 ** k
                src, dst = CAB[k % 2], CAB[(k + 1) % 2]
                sn, dn = ("CA", "CB")[k % 2], ("CA", "CB")[(k + 1) % 2]
                sn4 = [(sn, a_, b_) for a_ in range(2) for b_ in range(2)]
                for step in range(4):
                    for dd in range(2):
                        lo, hi = (4 - sh, 12 - sh) if dd == 0 else (4 + sh, 12 + sh)
                        er, ei, nei = (CBm[:, gp, dd, k, j:j + 1] for j in range(3))
                        sr, si = src[:, dd, 0], src[:, dd, 1]
                        dr, di = dst[:, dd, 0, 4:12], dst[:, dd, 1, 4:12]
                        dnr, dni = (dn, dd, 0), (dn, dd, 1)
                        if step == 0:
                            kb.stt(dr, sr[:, lo:hi], er, sr[:, 4:12], MUL, ADD, sn4 + ["SM"], [dnr])
                        elif step == 1:
                            kb.stt(di, si[:, lo:hi], er, si[:, 4:12], MUL, ADD, sn4 + ["SM"], [dni])
                        elif step == 2:
                            kb.stt(dr, si[:, lo:hi], nei, dr, MUL, ADD, sn4 + ["SM", dnr], [dnr])
                        else:
                            kb.stt(di, sr[:, lo:hi], ei, di, MUL, ADD, sn4 + ["SM", dni], [dni])
            Il = CAB[1]
            kb.ts("dve", CIN[:, 0, :, 1:8], Il[:, 0, :, 4:11], flag, MUL, [("CB", a_, b_) for a_ in range(2) for b_ in range(2)] + ["flag"], ["CIN"])
            kb.cp("dve", CIN[:, 0, :, 0], H0[:, gp, 0, :], ["SM"], ["CIN"])
            kb.ts("dve", CIN[:, 1, :, 0:7], Il[:, 1, :, 5:12], flag, MUL, [("CB", a_, b_) for a_ in range(2) for b_ in range(2)] + ["flag"], ["CIN"])
            kb.cp("dve", CIN[:, 1, :, 7], H0[:, gp, 1, :], ["SM"], ["CIN"])
            s3 = [128, 8, 32]
            for dd in range(2):
                fr, fi = bc(FX[:, gp, dd, 0, :], s3, 1), bc(FX[:, gp, dd, 1, :], s3, 1)
                cr_, ci_ = bc(CIN[:, dd, 0, :], s3, 2), bc(CIN[:, dd, 1, :], s3, 2)
                hr, hi_ = Hl[:, dd, 0, :, 16:48], Hl[:, dd, 1, :, 16:48]
                for ix, (fa, ca, tgt, op, rix) in enumerate(((fr, cr_, hr, ADD, 0), (fr, ci_, hi_, ADD, 1),
                                                             (fi, ci_, hr, SUB, 0), (fi, cr_, hi_, ADD, 1))):
                    ft = FT[ix % 2]
                    kb.tt("dve", ft, fa, ca, MUL, ["SM", "CIN"], [("FT", ix % 2)])
                    kb.tt("dve", tgt, tgt, ft, op, [("SB", dd, rix), ("FT", ix % 2)], [("SB", dd, rix)])
            hp = HP[gp % 2]
            hn = ("HP", gp % 2)
            kb.cp("dve", hp[:, 0, :, :, 1:32], Hl[:, 0, :, :, 16:47], [("SB", a_, b_) for a_ in range(2) for b_ in range(2)], [hn])
            kb.cp("dve", hp[:, 0, :, :, 0], CIN[:, 0, :, :], ["CIN"], [hn])
            kb.cp("dve", hp[:, 1, :, :, 0:31], Hl[:, 1, :, :, 17:48], [("SB", a_, b_) for a_ in range(2) for b_ in range(2)], [hn])
            kb.cp("dve", hp[:, 1, :, :, 31], CIN[:, 1, :, :], ["CIN"], [hn])
            kb.cp("dve", ST[:, gp, 0, :, :], Hl[:, 0, :, :, 47], [("SB", a_, b_) for a_ in range(2) for b_ in range(2)], ["ST"])
            kb.cp("dve", ST[:, gp, 1, :, :], Hl[:, 1, :, :, 16], [("SB", a_, b_) for a_ in range(2) for b_ in range(2)], ["ST"])
            for g2 in range(2):
                g = 2 * gp + g2
                bank = 7 if g2 == 0 else 4
                rows = slice(64 * g2, 64 * g2 + 64)
                o = ps[:, bank, 0:256]
                kb.mm(o, MG[:, g, :], UG[:, g, :], True, False, ["MG", "UG"], [("ps", bank)])
                for dd in range(2):
                    for comp in range(2):
                        last = (dd == 1 and comp == 1)
                        lhsT = QD[rows, dd, gp, comp, :]
                        rhs = hp[rows, dd, comp].rearrange("p b i -> p (b i)")
                        if g2 == 0:
                            kb.mm(o, lhsT, rhs, False, last, [("QD", dd), hn], [("ps", bank)])
                        else:
                            kb.mm_hi(o, lhsT, rhs, False, last, [("QD", dd), hn], [("ps", bank)])
                ys = YS[g % 4]
                kb.act(ys, o, AF.Gelu, [("ps", bank)], [("YS", g % 4)])
                kb.dma("sp", d["y_scr"][g], ys, [("YS", g % 4)], [("y_scr", g)], "ys")
                src = d["y_scr"][g].rearrange("(t h) c -> h t c", h=16)
                dstv = GL[16 * (g % 8):16 * (g % 8) + 16, g // 8, :].rearrange("h (t c) -> h t c", t=8)
                kb.dma("sp", dstv, src, [("y_scr", g)], [("GL", g // 8, b) for b in range(4)], "gl")
        kb.dma("sp", d["st_out"][l], ST.rearrange("p g d r b -> p (g d r b)"), ["ST"], [("st_out", l)], "out")
        if l == 0:
            dbg("gl", GL, [4, NT], BF16, [("GL", k, b) for k in range(4) for b in range(4)])
        P.barrier()
        AR.reset(mk2)
        SIG = A([4, 512], BF16)
        WG = A([4, 512], BF16)
        kb.dma("pool", WG, d["w_glu"][l].rearrange("(k p) n -> p k n", p=128), [], ["WG"], "WG")
        for blk in range(4):
            cs = slice(blk * 512, blk * 512 + 512)
            for m in range(4):
                bank = 1 + m % 2
                for k in range(4):
                    kb.mm(ps[:, bank, :], WG[:, k, m * 128:(m + 1) * 128], GL[:, k, cs], k == 0, k == 3,
                          ["WG", ("GL", k, blk)], [("ps", bank)])
                kb.act(SIG[:, m, :], ps[:, bank, :], AF.Sigmoid, [("ps", bank), "bglu"], [("SIG", m)],
                       bias=bglu[:, l, m:m + 1])
            for m in range(4):
                kb.tt("dve", GL[:, m, cs], GL[:, m, cs], SIG[:, m, :], MUL, [("GL", m, blk), ("SIG", m)], [("GL", m, blk)])
        if l == 0:
            dbg("so", GL, [4, NT], BF16, [("GL", k, b) for k in range(4) for b in range(4)])
        P.barrier()
        AR.reset(mk)

    def phase_att(l, ATT, hook):
        mk = AR.mark()
        A = AR.alloc
        HB = A([8, 512], BF16)
        WQ = A([8, 768], BF16)
        QT = A([8, 512], BF16, parts=80)
        KT = A([2, NKEY], BF16, parts=80)
        VA = A([20, 2, 80], BF16)
        PT = [A([2, 512], BF16) for _ in range(2)]
        sqh = A([512], BF16, parts=64)
        lnh = A([512], F32, parts=64)
        rsh = A([512], F32, parts=64)
        qn = A([512], F32, parts=64)
        qnb = A([512], BF16, parts=64)
        t1 = A([512], F32, parts=64)
        t2 = A([512], F32, parts=64)
        ko = A([512], F32, parts=64)
        RC = A([512], F32, parts=64)
        RS = A([512], F32, parts=64)
        VO2 = A([512], F32)
        numS = VO2[0:64, :]
        VO = VO2[:, 0:128]
        rd = A([2, 512], F32)
        rhl = A([2, 512], BF16)
        kb.dma("pool", WQ, d["w_in"][l].rearrange("(k p) n -> p k n", p=128)[:, :, 0:768], [], ["WQ"], "WQ")
        for h in range(2):
            kb.dma("pool", KT[0:64, h, NT:NKEY], d["kcT"][l, 64 * h:64 * h + 64, :], [], [("KT", h, 4)], "KTc")
            kb.dma("pool", KT[64:80, h, :], d["kaug"], [], [("KTa", h)], "KTc")
        for h in range(2):
            kb.dma("pool", VA[:, 16:20, h, 0:64], d["vc"][l].rearrange("(t p) (h e) -> p t h e", p=128, h=2)[:, :, h, :], [],
                   [("VAc", h)], "VAc")
        kb.ms("dve", VA[:, :, :, 64:80], 1.0, [("VAo")])

        def head_post(psrc, bank, gcol, blk, is_k, dst, dstname):
            cs = slice(blk * 512, blk * 512 + 512)
            kb.act(sqh, psrc, AF.Square, [("ps", bank)], ["sqh"])
            kb.mm(ps[0:64, 2, :], ones_b[0:64, 0:64], sqh, True, True, ["ones_b", "sqh"], [("ps", 2)])
            kb.act(lnh, ps[0:64, 2, :], AF.Ln, [("ps", 2), "epsT"], ["lnh"], bias=epsT[0:64], scale=1.0 / 64)
            kb.act(rsh, lnh, AF.Exp, ["lnh"], ["rsh"], scale=-0.5)
            kb.stt(qn, psrc, qkg[:, l, gcol:gcol + 1], rsh, MUL, MUL, [("ps", bank), "qkg", "rsh"], ["qn"])
            kb.act(qnb, qn, AF.Copy, ["qn"], ["qnb"])
            kb.mm(ps[0:64, 3, :], psw_b, qnb, True, True, ["psw_b", "qnb"], [("ps", 3)])
            kb.tt("dve", t1, qn, RC, MUL, ["qn", "RC"], ["t1"])
            kb.tt("dve", t2, ps[0:64, 3, :], RS, MUL, [("ps", 3), "RC"], ["t2"])
            if is_k:
                kb.tt("dve", ko, t1, t2, ADD, ["t1", "t2"], ["ko"])
                kb.act(dst, ko, AF.Copy, ["ko"], [dstname])
            else:
                kb.tt("dve", dst, t1, t2, ADD, ["t1", "t2"], [dstname])

        def load_rope(blk):
            cs = slice(blk * 512, blk * 512 + 512)
            kb.dma("sp", RC, d["ropeC"][:, cs], [], ["RC"], "rope")
            kb.dma("sp", RS, d["ropeS"][:, cs], [], ["RC"], "rope")

        for blk in range(4):
            cs = slice(blk * 512, blk * 512 + 512)
            kb.dma("sp", HB.rearrange("p k n -> p (k n)"), d["hb_scr"][blk], [("hb_scr", blk)], [("HB", k) for k in range(8)], "hbr")
            load_rope(blk)
            for h in range(2):
                proj(HB, "HB", WQ, "WQ", 512 + 64 * h, 64, ps[0:64, 1, :], 1)
                head_post(ps[0:64, 1, :], 1, 1, blk, True, KT[0:64, h, cs], ("KT", h, blk))
                kb.dma("sp", d["kT_out"][l, 64 * h:64 * h + 64, cs], ko, ["ko"], [("kT_out", l, h, blk)], "out")
            for tt_ in range(4):
                kt = 4 * blk + tt_
                for k in range(8):
                    kb.mm(ps[:, 3, 0:128], HB[:, k, tt_ * 128:(tt_ + 1) * 128], WQ[:, k, 640:768], k == 0, k == 7,
                          [("HB", k), "WQ"], [("ps", 3)])
                kb.act(VO, ps[:, 3, 0:128], AF.Copy, [("ps", 3)], ["numS"])
                kb.cp("dve", VA[:, kt, :, 0:64], VO.rearrange("p (h e) -> p h e", h=2), ["numS"], [("VA", kt)])
                kb.dma("sp", d["v_out"][l, kt * 128:(kt + 1) * 128, :], VO, ["numS"], [("v_out", l, kt)], "out")
        def q_stages(h, blk):
            cs = slice(blk * 512, blk * 512 + 512)
            psrc = ps[0:64, 0, :]

            def s0():
                kb.dma("pool", QT[64:80, h, :], d["qaug"][:, cs], [], [("QTa", h)], "QTa")
                proj(HB, "HB", WQ, "WQ", 64 * h, 64, psrc, 0)
                kb.act(sqh, psrc, AF.Square, [("ps", 0)], ["sqh"])

            def s1():
                kb.mm(ps[0:64, 1, :], ones_b[0:64, 0:64], sqh, True, True, ["ones_b", "sqh"], [("ps", 1)])
                kb.act(lnh, ps[0:64, 1, :], AF.Ln, [("ps", 1), "epsT"], ["lnh"], bias=epsT[0:64], scale=1.0 / 64)
                kb.act(rsh, lnh, AF.Exp, ["lnh"], ["rsh"], scale=-0.5)
                kb.stt(qn, psrc, qkg[:, l, 0:1], rsh, MUL, MUL, [("ps", 0), "qkg", "rsh"], ["qn"])
                kb.stt(qnb, psrc, qkg[:, l, 0:1], rsh, MUL, MUL, [("ps", 0), "qkg", "rsh"], ["qnb"])

            def s2():
                kb.mm(ps[0:64, 1, :], psw_b, qnb, True, True, ["psw_b", "qnb"], [("ps", 1)])
                kb.tt("dve", t1, qn, RC, MUL, ["qn", "RC"], ["t1"])
                kb.tt("dve", t2, ps[0:64, 1, :], RS, MUL, [("ps", 1), "RC"], ["t2"])
                kb.tt("dve", QT[0:64, h, :], t1, t2, ADD, ["t1", "t2"], [("QT", h)])
            return [s0, s1, s2]

        def epilogue1(h, blk):
            kb.act(rd[64:65, 0, :], ps[64:65, 6, :], AF.Ln, [("ps", 6)], ["rd0"])
            kb.act(rd[64:65, 1, :], rd[64:65, 0, :], AF.Exp, ["rd0"], ["rd1"], scale=-1.0)
            kb.cp("dve", numS, ps[0:64, 6, :], [("ps", 6), "rd0"], ["numS"])
            kb.cp("dve", rhl[64:65, 0, :], rd[64:65, 1, :], ["rd1"], ["rh"])
            kb.tt("dve", rhl[64:65, 1, :], rd[64:65, 1, :], rhl[64:65, 0, :], SUB, ["rd1", "rh"], ["rl"])

        def epilogue2(h, blk):
            cs = slice(blk * 512, blk * 512 + 512)
            kb.mm(ps[0:64, 7, :], ones_b[64:65, 0:64], rhl[64:65, 0, :], True, False, ["ones_b", "rh"], [("ps", 7)])
            kb.mm(ps[0:64, 7, :], ones_b[64:65, 0:64], rhl[64:65, 1, :], False, True, ["ones_b", "rl"], [("ps", 7)])
            kb.tt("dve", ATT[:, h, cs], numS, ps[0:64, 7, :], MUL, ["numS", ("ps", 7)], [("ATT", h, blk)])

        unit = 0

        def load_block(blk):
            kb.dma("sp", HB.rearrange("p k n -> p (k n)"), d["hb_scr"][blk], [("hb_scr", blk)], [("HB", k) for k in range(8)], "hbr")
            load_rope(blk)

        pend_epi = None
        for blk in range(4):
            if blk == 0:
                load_block(0)
                for st_ in q_stages(0, 0):
                    st_()
            for h in range(8):
                kvh = h // 4
                if h < 7:
                    nxt = q_stages(h + 1, blk)
                elif blk < 3:
                    load_block(blk + 1)
                    nxt = q_stages(0, blk + 1)
                else:
                    nxt = []
                sched = {}
                if nxt:
                    sched[1], sched[4], sched[7] = nxt[0], nxt[1], nxt[2]
                kdeps = [("KT", kvh, b) for b in range(5)] + [("KTa", kvh)]
                def qk2(kp):
                    b0 = 2 + 2 * (kp % 2)
                    for j in range(2):
                        kt = 2 * kp + j
                        kb.mm(ps[:, b0 + j, :], KT[0:80, kvh, kt * 128:(kt + 1) * 128], QT[0:80, h, :], True, True,
                              kdeps + [("QT", h), ("QTa", h)], [("ps", b0 + j)])
                    kb.act(PT[kp % 2], ps[:, b0:b0 + 2, :], AF.Exp, [("ps", b0), ("ps", b0 + 1)], [("PT", kp % 2)], scale=0.125)

                qk2(0)
                for kp in range(10):
                    if kp + 1 < 10:
                        qk2(kp + 1)
                    for j in range(2):
                        kt = 2 * kp + j
                        kb.mm(ps[0:65, 6, :], VA[:, kt, kvh, 0:65], PT[kp % 2][:, j, :], kt == 0, kt == 19,
                              [("VA", kt), "VAo", ("VAc", 0), ("VAc", 1), ("PT", kp % 2)], [("ps", 6)])
                    if kp == 2 and pend_epi is not None:
                        epilogue2(*pend_epi)
                        pend_epi = None
                    if kp in sched:
                        sched[kp]()
                epilogue1(h, blk)
                pend_epi = (h, blk)
                hook(unit)
                unit += 1
        epilogue2(*pend_epi)
        P.barrier()
        AR.reset(mk)

    def phase_out(l, ATT):
        mk = AR.mark()
        WOa = AR.alloc([8, 1024], BF16, parts=64)
        WOs = AR.alloc([4, 1024], BF16)
        kb.dma("pool", WOa, d["w_out"][l, 0:512, :].rearrange("(h e) n -> e h n", e=64), [], ["WOa"], "WOa")
        kb.dma("pool", WOs, d["w_out"][l, 512:1024, :].rearrange("(k p) n -> p k n", p=128), [], ["WOs"], "WOs")
        for blk in range(4):
            cs = slice(blk * 512, blk * 512 + 512)
            for m in range(8):
                bank = m % 2
                for h in range(8):
                    kb.mm(ps[:, bank, :], WOa[:, h, m * 128:(m + 1) * 128], ATT[:, h, cs], h == 0, False,
                          ["WOa", ("ATT", h, blk)], [("ps", bank)])
                for k in range(4):
                    kb.mm(ps[:, bank, :], WOs[:, k, m * 128:(m + 1) * 128], GL[:, k, cs], False, k == 3,
                          ["WOs", ("GL", k, blk)], [("ps", bank)])
                kb.stt(X[:, m, cs], ps[:, bank, :], DER[:, l, 2, m:m + 1], X[:, m, cs], MUL, ADD,
                       [("ps", bank), ("DER", l), Xn(m, blk)], [Xn(m, blk)])
        P.barrier()
        AR.reset(mk)

    def phase_ffn(l):
        mk = AR.mark()
        A = AR.alloc
        T = norm_tmps()
        H2 = A([2, 8, 512], BF16)
        HH = A([22, 1024], BF16)
        W1 = [A([8, 2, 256], BF16), A([8, 2, 256], BF16)]
        W2 = [A([22, 256], BF16), A([22, 256], BF16)]
        SG = [A([512], BF16), A([512], BF16)]
        w1v = d["w_ffn_in"][l].rearrange("(k p) n -> p k n", p=128)
        w2v = d["w_ffn_out"][l].rearrange("(k p) n -> p k n", p=128)
        cnt = 0
        for hf in range(2):
            for sub in range(2):
                norm_block(l, 1, 2 * hf + sub, H2[:, sub], ("H2", sub), T)
            for fg in range(11):
                ntile = 2
                w = W1[fg % 2]
                wn = ("W1", fg % 2)
                wd = ntile * 128
                kb.dma("pool", w[:, :, 0, 0:wd], w1v[:, :, fg * 256:fg * 256 + wd], [], [wn], wn)
                kb.dma("pool", w[:, :, 1, 0:wd], w1v[:, :, D_FF + fg * 256:D_FF + fg * 256 + wd], [], [wn], wn)
                for ft in range(ntile):
                    f = fg * 2 + ft
                    for sub in range(2):
                        gb, ub = (0, 1) if cnt % 2 == 0 else (2, 3)
                        for k in range(8):
                            kb.mm(ps[:, gb, :], w[:, k, 0, ft * 128:(ft + 1) * 128], H2[:, sub, k, :], k == 0, k == 7,
                                  [wn, (("H2", sub), k)], [("ps", gb)])
                        for k in range(8):
                            kb.mm(ps[:, ub, :], w[:, k, 1, ft * 128:(ft + 1) * 128], H2[:, sub, k, :], k == 0, k == 7,
                                  [wn, (("H2", sub), k)], [("ps", ub)])
                        sg = SG[cnt % 2]
                        kb.act(sg, ps[:, gb, :], AF.Silu, [("ps", gb)], [("SG", cnt % 2)])
                        kb.tt("dve", HH[:, f, sub * 512:(sub + 1) * 512], sg, ps[:, ub, :], MUL,
                              [("SG", cnt % 2), ("ps", ub)], [("HH", f, sub)])
                        cnt += 1
            for mp in range(4):
                w = W2[mp % 2]
                wn = ("W2", mp % 2)
                kb.dma("pool", w, w2v[:, :, mp * 256:(mp + 1) * 256], [], [wn], wn)
                for mi in range(2):
                    m = 2 * mp + mi
                    for sub in range(2):
                        blk = 2 * hf + sub
                        cs = slice(blk * 512, blk * 512 + 512)
                        bank = 4 + (2 * mi + sub) % 4
                        for f in range(22):
                            kb.mm(ps[:, bank, :], w[:, f, mi * 128:(mi + 1) * 128], HH[:, f, sub * 512:(sub + 1) * 512],
                                  f == 0, f == 21, [wn, ("HH", f, sub)], [("ps", bank)])
                        kb.stt(X[:, m, cs], ps[:, bank, :], DER[:, l, 5, m:m + 1], X[:, m, cs], MUL, ADD,
                               [("ps", bank), ("DER", l), Xn(m, blk)], [Xn(m, blk)])
        P.barrier()
        AR.reset(mk)

    for l in range(DEPTH):
        phase_ssm(l)
        mk = AR.mark()
        ATT = AR.alloc([8, NT], BF16, parts=64)
        if l == 0:
            WM1 = [AR.alloc([8, 256], BF16), AR.alloc([8, 256], BF16)]
            pieces1, fin1 = mods_pieces(1, WM1, 256)

            def hook(u):
                if u == 0:
                    pieces1[0][0]()
                    pieces1[1][0]()
                if u < 24:
                    pieces1[u][1]()
                    if u + 2 < 24:
                        pieces1[u + 2][0]()
                if u == 24:
                    fin1()
        else:
            def hook(u):
                pass
        phase_att(l, ATT, hook)
        if l == 0:
            dbg("att", ATT, [8, NT], BF16, [("ATT", h, b) for h in range(8) for b in range(4)], parts=64)
        phase_out(l, ATT)
        if l == 0:
            dbg("x_mid", X, [8, NT], F32, [Xn(k, b) for k in range(8) for b in range(4)])
        AR.reset(mk)
        phase_ffn(l)
        if l == 0:
            dbg("x1", X, [8, NT], F32, [Xn(k, b) for k in range(8) for b in range(4)])

    T = norm_tmps()
    YO = [AR.alloc([512], F32), AR.alloc([512], F32)]
    yv = d["yT"].rearrange("(k p) n -> p k n", p=128)
    for blk in range(4):
        cs = slice(blk * 512, blk * 512 + 512)
        for k in range(8):
            sq = T["sq"][k % 2]
            kb.act(sq, X[:, k, cs], AF.Square, [Xn(k, blk)], [("sq", k % 2)])
            kb.mm(ps[:, 0, :], ones_b, sq, k == 0, k == 7, ["ones_b", ("sq", k % 2)], [("ps", 0)])
        kb.act(T["lnv"], ps[:, 0, :], AF.Ln, [("ps", 0), "epsT"], ["lnv"], bias=epsT, scale=1.0 / D_MODEL)
        kb.act(T["rstd"], T["lnv"], AF.Exp, ["lnv"], ["rstd"], scale=-0.5)
        for k in range(8):
            yo = YO[k % 2]
            kb.stt(yo, X[:, k, cs], fnorm[:, k:k + 1], T["rstd"], MUL, MUL, [Xn(k, blk), "fnorm", "rstd"], [("YO", k % 2)])
            kb.dma("sp", yv[:, k, cs], yo, [("YO", k % 2)], [("yT", k, blk)], "out")
    return kb, xoff


def _perm():
    j = np.arange(NT)
    return 8 * (j % NCH) + j // NCH


def _rope_tables():
    n = np.arange(NT, dtype=np.float32)
    row = np.floor(n / 64.0).astype(np.float32)
    col = (n - 64.0 * row).astype(np.float32)
    inv = (10000.0 ** (-np.arange(0, 32, 2, dtype=np.float32) / 32.0)).astype(np.float32)
    ang = np.concatenate([row[:, None] * inv, col[:, None] * inv], axis=-1)
    cos, sin = np.cos(ang).astype(np.float32), np.sin(ang).astype(np.float32)
    C = np.repeat(cos, 2, axis=1)
    S = np.repeat(sin, 2, axis=1)
    S[:, 0::2] *= -1.0
    return C, S


def prep_inputs(inp):
    f = lambda a: np.ascontiguousarray(a, dtype=np.float32)
    perm = _perm()
    L = DEPTH

    def gpl(a):
        return a

    def a_lay(a):
        x = a.reshape(L, 2, 16, 2, 64)
        return f(x.transpose(0, 1, 3, 4, 2).reshape(L, 2, 128, 16))

    def b_lay(b):
        x = b.reshape(L, 2, 16, 2, 64, 16)
        return f(x.transpose(0, 1, 3, 4, 2, 5).reshape(L, 2, 128, 16, 16))

    def c_lay(c):
        x = c.reshape(L, 2, 16, 2, 16, 64)
        return f(x.transpose(0, 1, 3, 5, 2, 4).reshape(L, 2, 128, 16, 16))

    ldt = np.broadcast_to(inp["ssm_log_dt"][:, :, :, None], (L, 2, 32, 64))
    shared = {
        "w_mod": f(inp["w_mod"]),
        "b_modT": f(inp["b_mod"].reshape(L, 48, 128).transpose(0, 2, 1)),
        "norm1T": f(inp["norm1"].reshape(L, 8, 128).transpose(0, 2, 1)),
        "norm2T": f(inp["norm2"].reshape(L, 8, 128).transpose(0, 2, 1)),
        "fnormT": f(inp["final_norm"].reshape(8, 128).T),
        "w_in": f(inp["w_in"]),
        "qkg": f(np.stack([inp["q_norm"], inp["k_norm"]], axis=-1)),
        "a_reT": a_lay(inp["ssm_a_re"]), "a_imT": a_lay(inp["ssm_a_im"]), "ldtT": a_lay(ldt),
        "b_reT": b_lay(inp["ssm_b_re"]), "b_imT": b_lay(inp["ssm_b_im"]),
        "c_reT": c_lay(inp["ssm_c_re"]), "c_imT": c_lay(inp["ssm_c_im"]),
        "dS": f(np.broadcast_to(inp["ssm_d"].reshape(L, 1, 32, 16).transpose(0, 1, 3, 2), (L, 8, 16, 32)).reshape(L, 128, 32)),
        "w_glu": f(inp["w_glu"]),
        "b_gluT": f(inp["b_glu"].reshape(L, 4, 128).transpose(0, 2, 1)),
        "w_out": f(inp["w_out"]), "w_ffn_in": f(inp["w_ffn_in"]), "w_ffn_out": f(inp["w_ffn_out"]),
        "ident": np.eye(128, dtype=np.float32),
    }
    s_idx = np.arange(128) // 16
    shared["maskL"] = (s_idx[:, None] <= s_idx[None, :]).astype(np.float32)
    shared["maskU"] = (s_idx[:, None] >= s_idx[None, :]).astype(np.float32)
    psw = np.zeros((64, 64), np.float32)
    psw[np.arange(64), np.arange(64) ^ 1] = 1.0
    shared["psw"] = psw
    C, S = _rope_tables()
    in_maps = []
    for ci in range(8):
        m = dict(shared)
        sample = ci < 4
        if sample:
            xc = inp["x_sample"][ci]
            cvec = inp["c"][ci]
        else:
            xc = inp["x_prompt"][8 * (ci - 4):8 * (ci - 4) + 8].reshape(NT, D_MODEL)
            cvec = inp["c_ctx"]
        m["xT"] = f(xc[perm].T)
        m["cond"] = f(cvec.reshape(8, 128).T)
        m["flag"] = np.full((128, 1), 1.0 if sample else 0.0, np.float32)
        qaug = np.zeros((16, NT), np.float32)
        kaug = np.zeros((16, NKEY), np.float32)
        if sample:
            st = inp["state_ssm"][ci]
            x = st.reshape(L, 2, 2, 16, 2, 64)
            m["h0T"] = f(x.transpose(0, 4, 5, 3, 1, 2).reshape(L, 128, 64))
            m["ropeC"], m["ropeS"] = f(C[perm].T), f(S[perm].T)
            kaug[0, :NT] = 1.0
            kaug[8, NT:] = 1.0
            m["kcT"] = f(inp["cache_k"][ci].reshape(L, 512, 128).transpose(0, 2, 1))
            m["vc"] = f(inp["cache_v"][ci].reshape(L, 512, 128))
        else:
            m["h0T"] = np.zeros((L, 128, 64), np.float32)
            m["ropeC"], m["ropeS"] = np.ones((64, NT), np.float32), np.zeros((64, NT), np.float32)
            seg = perm // 256
            kaug[seg, np.arange(NT)] = 1.0
            kaug[8, NT:] = 1.0
            qaug[:8, :] = -BIG
            qaug[seg, np.arange(NT)] = 0.0
            qaug[8, :] = -BIG
            m["kcT"] = np.zeros((L, 128, 512), np.float32)
            m["vc"] = np.zeros((L, 512, 128), np.float32)
        m["qaug"], m["kaug"] = qaug, kaug
        in_maps.append(m)
    return in_maps


_CACHE = {}


def kernel(**inputs):
    inp = {k: np.asarray(v) for k, v in inputs.items()}
    in_maps = prep_inputs(inp)
    kb, _ = build()
    kb.P.emit()
    res = run_bass_kernel_spmd(kb.nc, in_maps, core_ids=list(range(8)))
    R = res.results
    perm = _perm()
    inv = np.empty(NT, np.int64)
    inv[perm] = np.arange(NT)
    y_sample = np.zeros((4, NT, D_MODEL), np.float32)
    y_prompt = np.zeros((32, 256, D_MODEL), np.float32)
    new_k = np.zeros((32, DEPTH, 256, 2, 64), np.float32)
    new_v = np.zeros((32, DEPTH, 256, 2, 64), np.float32)
    new_s = np.zeros((32, DEPTH, 2, 2, 32, 64), np.float32)
    for ci in range(8):
        r = R[ci]
        y = np.asarray(r["yT"]).T[inv]
        if ci < 4:
            y_sample[ci] = y
        else:
            b0 = 8 * (ci - 4)
            y_prompt[b0:b0 + 8] = y.reshape(8, 256, D_MODEL)
            kT = np.asarray(r["kT_out"])
            vv = np.asarray(r["v_out"])
            st = np.asarray(r["st_out"]).reshape(DEPTH, 2, 64, 16, 2, 2, 8)
            for l in range(DEPTH):
                new_k[b0:b0 + 8, l] = kT[l].T[inv].reshape(8, 256, 2, 64)
                new_v[b0:b0 + 8, l] = vv[l][inv].reshape(8, 256, 2, 64)
            new_s[b0:b0 + 8] = st.transpose(6, 0, 4, 5, 3, 1, 2).reshape(8, DEPTH, 2, 2, 32, 64)
    return (y_prompt, y_sample, new_k, new_v, new_s)
```
